# Optimizing a Trainium2 kernel written in Bass

```python
import jax, jax.numpy as jnp
from jax import lax
import numpy as np

D_MODEL = 1024
BATCH = 8
SEQ = 2048
DEPTH = 1
DEC_BATCH = 128
DEC_SEQ = 8
PAST_LEN = 16384
PAGE_SIZE = 128

MIX_WIDTH = D_MODEL
SGU_WIDTH = MIX_WIDTH // 2
SGU_GROUPS = 8
SGU_GROUP_DIM = SGU_WIDTH // SGU_GROUPS
SGU_CHUNK = 128
RET_WIDTH = MIX_WIDTH - SGU_WIDTH
RET_HEADS = 4
RET_DK = RET_WIDTH // RET_HEADS
RET_DV = RET_WIDTH // RET_HEADS
RET_CHUNK = 128
ROPE_THETA = 10000.0
PLE_DIM = 256
RMS_EPS = 1e-6
LN_EPS = 1e-5
IN_WIDTH = 3 * SGU_WIDTH + RET_HEADS * (2 * RET_DK + 2 * RET_DV)

kernel_name = "hybrid_sgu_retention_decode_step"


def rms_norm(x, g):
    xf = x.astype(jnp.float32)
    y = xf * lax.rsqrt(jnp.mean(xf * xf, axis=-1, keepdims=True) + RMS_EPS)
    return (y * g.astype(jnp.float32)).astype(x.dtype)


def layer_norm(x, g):
    xf = x.astype(jnp.float32)
    mu = jnp.mean(xf, axis=-1, keepdims=True)
    var = jnp.mean(jnp.square(xf - mu), axis=-1, keepdims=True)
    return ((xf - mu) * lax.rsqrt(var + LN_EPS) * g.astype(jnp.float32)).astype(x.dtype)


def rotary(x, pos):
    half = x.shape[-1] // 2
    inv = ROPE_THETA ** (-jnp.arange(half, dtype=jnp.float32) / half)
    ang = pos[:, None] * inv[None, :]
    cos = jnp.cos(ang)[None, :, None, :]
    sin = jnp.sin(ang)[None, :, None, :]
    x1, x2 = x[..., :half], x[..., half:]
    return jnp.concatenate([x1 * cos - x2 * sin, x2 * cos + x1 * sin], axis=-1)


def retention(q, k, v, s0):
    b, L, h, dk = q.shape
    dv = v.shape[-1]
    c = min(RET_CHUNK, L)
    n = L // c
    log_g = jnp.log(1.0 - 2.0 ** (-5.0 - jnp.arange(RET_HEADS, dtype=jnp.float32)))
    idx = jnp.arange(c, dtype=jnp.float32)
    rel = idx[:, None] - idx[None, :]
    decay = jnp.where(rel >= 0, jnp.exp(log_g[:, None, None] * jnp.maximum(rel, 0.0)), 0.0)
    qc = q.reshape(b, n, c, h, dk)
    kc = k.reshape(b, n, c, h, dk)
    vc = v.reshape(b, n, c, h, dv)
    scores = jnp.einsum('bnihd,bnjhd->bnhij', qc, kc) * decay[None, None]
    o_intra = jnp.einsum('bnhij,bnjhe->bnihe', scores, vc)
    w_kv = jnp.exp(log_g[:, None] * (c - 1.0 - idx)[None, :])
    u = jnp.einsum('bnjhd,bnjhe,hj->bnhde', kc, vc, w_kv)
    g_chunk = jnp.exp(log_g * c)[:, None, None]

    def step(s, u_n):
        return g_chunk * s + u_n, s

    s_final, s_prev = lax.scan(step, s0, jnp.moveaxis(u, 1, 0))
    s_prev = jnp.moveaxis(s_prev, 0, 1)
    w_q = jnp.exp(log_g[:, None] * (idx + 1.0)[None, :])
    o_inter = jnp.einsum('bnihd,bnhde,hi->bnihe', qc, s_prev, w_q)
    return (o_intra + o_inter).reshape(b, L, h, dv), s_final


def spatial_gating(u, v, w_s, b_s, ln_g):
    b, L, _ = u.shape
    c = min(SGU_CHUNK, L)
    n = L // c
    v = layer_norm(v, ln_g)
    w = jnp.tril(w_s[:, :c, :c])
    vc = v.reshape(b, n, c, SGU_GROUPS, SGU_GROUP_DIM)
    mixed = jnp.einsum('gij,bnjgc->bnigc', w, vc) + jnp.transpose(b_s[:, :c])[None, None, :, :, None]
    return u * mixed.reshape(b, L, SGU_WIDTH), v


def hybrid_layer(x, p, s0, pos, w_in, w_out, norm_pre, norm_post, sgu_w, sgu_b, sgu_ln,
                 ret_gn, w_ple_proj, w_ple_gate):
    b, L, _ = x.shape
    h = rms_norm(x, norm_pre)
    z = h @ w_in
    o1 = SGU_WIDTH
    o2 = o1 + SGU_WIDTH
    o3 = o2 + SGU_WIDTH
    o4 = o3 + RET_HEADS * RET_DK
    o5 = o4 + RET_HEADS * RET_DK
    o6 = o5 + RET_HEADS * RET_DV
    su = jax.nn.gelu(z[..., :o1])
    sv = jax.nn.gelu(z[..., o1:o2])
    sg = z[..., o2:o3]
    rq = z[..., o3:o4].reshape(b, L, RET_HEADS, RET_DK).astype(jnp.float32)
    rk = z[..., o4:o5].reshape(b, L, RET_HEADS, RET_DK).astype(jnp.float32)
    rv = z[..., o5:o6].reshape(b, L, RET_HEADS, RET_DV).astype(jnp.float32)
    rg = z[..., o6:]
    a_out, v_rows = spatial_gating(su, sv, sgu_w, sgu_b, sgu_ln)
    a_out = jax.nn.silu(sg) * a_out
    q = rotary(rq, pos)
    k = rotary(rk, pos) * (RET_DK ** -0.5)
    o, s_final = retention(q, k, rv, s0.astype(jnp.float32))
    mu = jnp.mean(o, axis=-1, keepdims=True)
    var = jnp.mean(jnp.square(o - mu), axis=-1, keepdims=True)
    o = (o - mu) * lax.rsqrt(var + LN_EPS) * ret_gn.astype(jnp.float32).reshape(RET_HEADS, RET_DV)
    b_out = jax.nn.silu(rg) * o.reshape(b, L, RET_HEADS * RET_DV).astype(x.dtype)
    mix = jnp.concatenate([a_out, b_out], axis=-1) @ w_out
    x = x + rms_norm(mix, norm_post)
    x = x + jax.nn.sigmoid(x @ w_ple_gate) * (p @ w_ple_proj)
    return x, s_final.astype(s0.dtype), v_rows


def setup_inputs(seed: int = 0) -> dict:
    key = jax.random.key(seed)
    ks = jax.random.split(key, 16)
    f32 = jnp.float32
    nrm = lambda k, shape, s: jax.random.normal(k, shape, f32) * s
    return {
        "x_prompt": nrm(ks[0], (BATCH, SEQ, D_MODEL), 1.0),
        "x_sample": nrm(ks[1], (DEC_BATCH, DEC_SEQ, D_MODEL), 1.0),
        "state_ret": nrm(ks[2], (DEPTH, DEC_BATCH, RET_HEADS, RET_DK, RET_DV), 0.5),
        "p_prompt": nrm(ks[3], (DEPTH, BATCH, SEQ, PLE_DIM), 1.0),
        "p_sample": nrm(ks[4], (DEPTH, DEC_BATCH, DEC_SEQ, PLE_DIM), 1.0),
        "w_in": nrm(ks[5], (DEPTH, D_MODEL, IN_WIDTH), D_MODEL ** -0.5),
        "w_out": nrm(ks[6], (DEPTH, MIX_WIDTH, D_MODEL), MIX_WIDTH ** -0.5),
        "norm_pre": 1.0 + nrm(ks[7], (DEPTH, D_MODEL), 0.05),
        "norm_post": 1.0 + nrm(ks[8], (DEPTH, D_MODEL), 0.05),
        "sgu_w": nrm(ks[9], (DEPTH, SGU_GROUPS, SGU_CHUNK, SGU_CHUNK), SGU_CHUNK ** -0.5),
        "sgu_b": 1.0 + nrm(ks[10], (DEPTH, SGU_GROUPS, SGU_CHUNK), 0.1),
        "sgu_ln": 1.0 + nrm(ks[11], (DEPTH, SGU_WIDTH), 0.05),
        "ret_gn": 1.0 + nrm(ks[12], (DEPTH, RET_HEADS * RET_DV), 0.05),
        "w_ple_proj": nrm(ks[13], (DEPTH, PLE_DIM, D_MODEL), PLE_DIM ** -0.5),
        "w_ple_gate": nrm(ks[14], (DEPTH, D_MODEL, D_MODEL), D_MODEL ** -0.5),
    }


def reference(x_prompt, x_sample, state_ret, p_prompt, p_sample, w_in, w_out, norm_pre,
              norm_post, sgu_w, sgu_b, sgu_ln, ret_gn, w_ple_proj, w_ple_gate):
    seq = x_prompt.shape[1]
    dec_seq = x_sample.shape[1]
    pos_prompt = jnp.arange(seq, dtype=jnp.float32)
    pos_sample = jnp.arange(dec_seq, dtype=jnp.float32) + PAST_LEN
    s0_prompt = jnp.zeros((x_prompt.shape[0], RET_HEADS, RET_DK, RET_DV), state_ret.dtype)
    xp, xs = x_prompt, x_sample
    st_p, st_s, v_s = [], [], []
    for l in range(DEPTH):
        wl = (w_in[l], w_out[l], norm_pre[l], norm_post[l], sgu_w[l], sgu_b[l], sgu_ln[l],
              ret_gn[l], w_ple_proj[l], w_ple_gate[l])
        xp, sp, _ = hybrid_layer(xp, p_prompt[l], s0_prompt, pos_prompt, *wl)
        xs, ss, vs = hybrid_layer(xs, p_sample[l], state_ret[l], pos_sample, *wl)
        st_p.append(sp)
        st_s.append(ss)
        v_s.append(vs)
    new_state_ret_prompt = jnp.stack(st_p, axis=0)
    new_state_ret_sample = jnp.stack(st_s, axis=0)
    new_sgu_v_sample = jnp.stack(v_s, axis=0)
    return (xp, xs, new_state_ret_prompt, new_state_ret_sample, new_sgu_v_sample)
```

```python
import numpy as np
from contextlib import ExitStack
import concourse.bass as bass
import concourse.mybir as mybir
from concourse.bass_utils import run_bass_kernel_spmd

F32 = mybir.dt.float32
BF16 = mybir.dt.bfloat16
ALU = mybir.AluOpType
AF = mybir.ActivationFunctionType

NCORES = 8
D = 1024
SEQ = 2048
NCH = 16
INW = 3584
PLE = 256
PAST = 16384
RMS_EPS = 1e-6
LN_EPS = 1e-5
GAM = [1.0 - 2.0 ** (-5.0 - h) for h in range(4)]

COMPUTE = ("pe", "act", "dve", "pool")
PSUM_BANKS = ("P01", "P2", "P3", "P4", "P5", "P6", "P7")


class Buf:
    __slots__ = ("name", "w", "r")

    def __init__(self, name):
        self.name = name
        self.w = None
        self.r = []


class Ev:
    __slots__ = ("eng", "idx", "sem", "val", "clock", "needed", "is_dma", "t_done")

    def __init__(self):
        self.eng = None
        self.idx = 0
        self.sem = None
        self.val = 0
        self.clock = None
        self.needed = False
        self.is_dma = False
        self.t_done = 0.0

    def key(self):
        return ("d", id(self.sem)) if self.is_dma else ("e", self.eng)


class Op:
    __slots__ = ("eng", "fn", "deps", "ev", "waits", "slot")


class Sched:
    def __init__(self, nc, stack):
        self.nc = nc
        self.stack = stack
        self.ops = []
        self.per_eng = {e: [] for e in COMPUTE + ("sp",)}
        self.slots = {}
        self.eng_sem = {}
        self.tail = None
        for e in COMPUTE:
            self.eng_sem[e] = stack.enter_context(nc.semaphore("c_" + e))

    def _slot(self, name):
        if name not in self.slots:
            sem = self.stack.enter_context(self.nc.semaphore("d_" + name))
            self.slots[name] = [sem, 0, []]
        return self.slots[name]

    def peek(self, eng, reads=(), writes=(), slot=None):
        return self.add(eng, None, reads, writes, slot, _peek=True)

    def add(self, eng, fn, reads=(), writes=(), slot=None, _peek=False):
        is_dma = slot is not None

        def _flat(xs):
            out = []
            for x in xs:
                if isinstance(x, (list, tuple)):
                    out.extend(_flat(x))
                elif x not in out:
                    out.append(x)
            return out

        reads = _flat(reads)
        writes = _flat(writes)
        deps = []
        for b in reads:
            if b.w is not None:
                deps.append((b.w, "raw"))
            if b.name in PSUM_BANKS:
                for e in b.r:
                    if e.eng != eng:
                        deps.append((e, "bank"))
        for b in writes:
            if b.w is not None:
                deps.append((b.w, "waw"))
            for e in b.r:
                deps.append((e, "war"))
        op = Op()
        op.eng = eng
        op.fn = fn
        op.slot = slot
        fd = []
        seen = set()
        for (e, kind) in deps:
            if id(e) in seen:
                continue
            if (not is_dma) and (not e.is_dma) and e.eng == eng and eng == "pe":
                continue
            seen.add(id(e))
            fd.append(e)
        if _peek:
            return fd
        op.deps = fd
        ev = Ev()
        ev.is_dma = is_dma
        ev.eng = eng
        self.per_eng[eng].append(op)
        if is_dma:
            s = self._slot(slot)
            s[1] += 1
            ev.sem = s[0]
            ev.val = 16 * s[1]
            s[2].append(ev)
        else:
            ev.idx = len(self.per_eng[eng])
            ev.sem = self.eng_sem[eng]
        op.ev = ev
        self.ops.append(op)
        for b in writes:
            b.w = ev
            b.r = []
        for b in reads:
            if b not in writes:
                b.r.append(ev)
        return ev

    def seal_group(self, slot):
        s = self.slots[slot]
        for ev in s[2]:
            ev.val = 16 * s[1]

    def finalize(self):
        known = {e: {} for e in self.per_eng}
        for op in self.ops:
            k = known[op.eng]
            waits = []
            for ev in op.deps:
                key = ev.key()
                val = ev.val if ev.is_dma else ev.idx
                if k.get(key, 0) >= val:
                    continue
                waits.append(ev)
                ev.needed = True
                for kk, vv in ev.clock.items():
                    if k.get(kk, 0) < vv:
                        k[kk] = vv
                if k.get(key, 0) < val:
                    k[key] = val
            op.waits = waits
            op.ev.clock = dict(k)
        for e in COMPUTE:
            c = 0
            for op in self.per_eng[e]:
                if op.ev.is_dma:
                    continue
                if op.ev.needed:
                    c += 1
                    op.ev.val = c

    def emit(self, block):
        sched = self

        def run(engname, engobj):
            for op in sched.per_eng[engname]:
                for ev in op.waits:
                    engobj.wait_ge(ev.sem, ev.val)
                inst = op.fn(engobj)
                if op.ev.is_dma:
                    inst.then_inc(op.ev.sem, 16)
                elif op.ev.needed:
                    inst.then_inc(op.ev.sem, 1)
            if sched.tail and sched.tail[0] == engname:
                for ev in sched.tail[1]:
                    engobj.wait_ge(ev.sem, ev.val)

        @block.sync
        def _(e):
            run("sp", e)

        @block.tensor
        def _(e):
            run("pe", e)

        @block.scalar
        def _(e):
            run("act", e)

        @block.vector
        def _(e):
            run("dve", e)

        @block.gpsimd
        def _(e):
            run("pool", e)


CF = {}
_off = 0
for _n, _w in [("ident", 128), ("wqt", 512), ("dd", 512),
               ("wkv", 4), ("wkvs", 4), ("bm", 16), ("tm", 128), ("tms", 128), ("g8", 512)]:
    CF[_n] = (_off, _w)
    _off += _w
CF_W = _off


def _host_consts():
    cf = np.zeros((128, CF_W), np.float32)
    cs = np.zeros((128, 1024), np.float32)

    def put(name, arr):
        a = np.asarray(arr, np.float64)
        if name == "wqts":
            cs[:, 0:512] = a.reshape(128, 512).astype(np.float32)
            return
        if name == "dds":
            cs[:, 512:1024] = a.reshape(128, 512).astype(np.float32)
            return
        o, w = CF[name]
        cf[:, o:o + w] = a.reshape(128, w).astype(np.float32)

    i = np.arange(128)
    put("ident", np.eye(128))
    g = np.array(GAM, np.float64)
    wqt = g[:, None] ** (i[None, :] + 1.0)
    put("wqt", np.broadcast_to(wqt[None], (128, 4, 128)))
    dd = (128.0 ** -0.5) * g[None, :, None] ** (-(i[:, None, None]) - 1.0) * (i[None, None, :] >= i[:, None, None])
    put("dd", dd)
    put("wkv", (128.0 ** -0.5) * g[None, :] ** (127.0 - i[:, None]))
    t8 = i % 8
    b8 = i // 8
    wqts = g[:, None] ** (t8[None, :] + 1.0)
    put("wqts", np.broadcast_to(wqts[None], (128, 4, 128)))
    same = (b8[:, None] == b8[None, :]) & (t8[None, :] >= t8[:, None])
    dds = (128.0 ** -0.5) * g[None, :, None] ** (-(t8[:, None, None]) - 1.0) * same[:, None, :]
    put("dds", dds)
    put("wkvs", (128.0 ** -0.5) * g[None, :] ** (7.0 - t8[:, None]))
    put("bm", (b8[:, None] == np.arange(16)[None, :]).astype(np.float64))
    put("tm", (i[None, :] >= i[:, None]).astype(np.float64))
    put("tms", same.astype(np.float64))
    put("g8", np.broadcast_to((g ** 8.0)[None, :, None], (128, 4, 128)))
    inv = 10000.0 ** (-(np.arange(64, dtype=np.float64)) / 64.0)

    def rope(pos):
        ang = pos.astype(np.float64)[:, None] * inv[None, :]
        c, s = np.cos(ang), np.sin(ang)
        return np.concatenate([c, c, -s, s], axis=1).astype(np.float32)

    rope_p = rope(np.arange(SEQ)).reshape(NCH, 128, 256)
    rope_s = rope(PAST + t8)
    rope_all = np.ascontiguousarray(np.concatenate([rope_p, rope_s[None]], axis=0))
    e40 = np.zeros((40, 512), np.float32)
    for gi in range(8):
        e40[gi, gi * 64:(gi + 1) * 64] = 1.0
        e40[32 + gi, gi * 64:(gi + 1) * 64] = 1.0
    return cf, cs, rope_all, e40


PSUM_BANKS = ("PA0", "PA1", "PZ0", "PZ1", "PS", "PT", "P6", "P7")
ZORDER = [3, 4, 5, 1, 0, 2, 6]
TRACE_SCHED = None
AGE_BIAS = 0.0
PE_STALL_PEN = 200.0
TIE = 40.0
SLACK = 1.0
SYNC_LAT = 100.0
SAMPLE_POS = 4
WINDOW = 2
DMA_BW = 330.0
DMA_LAT = 2000.0


def build_program(nch=NCH, with_sample=True, window=WINDOW):
    nc = bass.Bass("TRN2", target_bir_lowering=False)

    def din(name, shape):
        return nc.dram_tensor(name, list(shape), F32, kind="ExternalInput").ap()

    def dout(name, shape):
        return nc.dram_tensor(name, list(shape), F32, kind="ExternalOutput").ap()

    xp = din("xp", [SEQ, D])
    xs = din("xs", [128, D])
    pp = din("pp", [SEQ, PLE])
    psm = din("psm", [128, PLE])
    st_in = din("st", [16, 4, 128, 128])
    w_in = din("w_in", [D, INW])
    w_out = din("w_out", [D, D])
    w_gate = din("w_gate", [D, D])
    w_ple = din("w_ple", [PLE, D])
    gT_d = din("gT", [128, 8])
    gvec_d = din("gvec", [128, 2048])
    wsT_d = din("wsT", [128, 8, 128])
    wsTs_d = din("wsTs", [128, 8, 128])
    b40_d = din("b40", [40, 256])
    cf_d = din("cf", [128, CF_W])
    cs_d = din("cs", [128, 1024])
    rope_d = din("rope", [NCH + 1, 128, 256])
    e40_d = din("e40", [40, 512])

    yp = dout("yp", [SEQ, D])
    ys = dout("ys", [128, D])
    sp_out = dout("sp_out", [4, 128, 128])
    ss_out = dout("ss_out", [16, 4, 128, 128])
    v_out = dout("v_out", [128, 512])

    st = ExitStack()
    with st:
        def sb(name, shape, dt=F32):
            return st.enter_context(nc.sbuf_tensor("s_" + name, list(shape), dt))

        def psum(name, shape, dt=F32):
            return st.enter_context(nc.psum_tensor("ps_" + name, list(shape), dt))

        S = Sched(nc, st)
        bufs = {}
        ALIAS = {"vf": ["bufA"], "b40": ["bufA"], "b40f": ["bufA"], "b40h": ["bufA"], "PA": ["PA0", "PA1"], "P67": ["P6", "P7"],
                 "rtA": ["rtA0", "rtA1", "rtA2", "rtA3"], "rtB": ["rtB0", "rtB1"],
                 "ot0": ["rtA0"], "ot1": ["rtA1"], "ot2": ["rtA2"], "ot3": ["rtA3"],
                 "ot": ["rtA0", "rtA1", "rtA2", "rtA3"],
                 "gst": ["gst0", "gst1", "gst2", "gst3"], "gmv": ["gmv0", "gmv1", "gmv2", "gmv3"],
                 "Sst": ["Sst0", "Sst1", "Sst2", "Sst3"], "VM": ["VM0", "VM1", "VM2", "VM3"]}

        def B(name):
            if name in ALIAS:
                return [B(n) for n in ALIAS[name]]
            if name not in bufs:
                bufs[name] = Buf(name)
            return bufs[name]

        win = sb("win", [128, 8, INW], BF16)
        wout = sb("wout", [128, 8, D], BF16)
        wgate = sb("wgate", [128, 8, D], BF16)
        wple = sb("wple", [128, 2, D], BF16)
        shared = sb("shared", [128, 8192], F32)
        cf = sb("cf", [128, CF_W])
        gT = sb("gT", [128, 8])
        gvec = sb("gvec", [128, 2048])
        identb = sb("identb", [128, 128], BF16)
        wsT = sb("wsTb", [128, 8, 128], BF16)
        wsTs = sb("wsTsb", [128, 8, 128], BF16)
        e40 = sb("e40b", [40, 512], BF16)
        b40l = sb("b40l", [40, 256], BF16)
        mhalf = sb("mhalf", [128, 4])

        def cfs(name):
            o, w = CF[name]
            return cf[:, o:o + w]

        xb = [sb("x%d" % i, [128, D]) for i in range(3)]
        pb = [sb("p%d" % i, [128, PLE]) for i in range(2)]
        rb = [sb("rope%d" % i, [128, 256]) for i in range(2)]
        xTb = [sb("xT%d" % i, [128, 8, 128], BF16) for i in range(2)]
        su = sb("su", [128, 512])
        sv = sb("sv", [128, 512])
        sgs = sb("sgs", [128, 512])
        rgs = sb("rgs", [128, 512])
        rtA = sb("rtA", [128, 4, 128])
        ot = rtA
        rtB = sb("rtB", [128, 4, 128])
        Qrb = [sb("Qr%d" % i, [128, 4, 128], BF16) for i in range(2)]
        Krb = [sb("Kr%d" % i, [128, 4, 128], BF16) for i in range(2)]
        Vbb = [sb("Vb%d" % i, [128, 4, 128], BF16) for i in range(2)]
        Vh = sb("Vh", [128, 4, 128], BF16)
        QKT = sb("QKT", [128, 8, 128], BF16)
        scT = sb("scT", [128, 4, 128], BF16)
        vbf = sb("vbf", [128, 512], BF16)
        tsg = sb("tsg", [128, 512])
        mix = sb("mix", [128, D], BF16)
        mixT = sb("mixT", [128, 8, 128], BF16)
        G2 = sb("G2", [128, 512])
        bufA = sb("bufA", [128, D])
        b40 = bufA[0:40, 0:256]
        b40f = bufA[0:40, 256:512]
        b40h = bufA[:].bitcast(BF16)[0:40, 1024:1280]
        vf = bufA[:, 512:1024]
        x1T = sb("x1T", [128, 8, 128], BF16)
        pTb = [sb("pT%d" % i, [128, 2, 128], BF16) for i in range(3)]
        Sst = sb("Sst", [128, 4, 128])
        Sbf = sb("Sbf", [128, 4, 128], BF16)
        ssqb = [sb("ssq_%d" % i, [128, 1]) for i in range(2)]
        msb = [sb("ms_%d" % i, [128, 1]) for i in range(2)]
        rstdb = [sb("rstd_%d" % i, [128, 1]) for i in range(2)]
        ssq2 = sb("ssq2", [128, 1]); ms2 = sb("ms2", [128, 1]); rstd2 = sb("rstd2", [128, 1])
        st6 = sb("st6", [128, 6]); mv = sb("mv", [128, 2]); ve = sb("ve", [128, 1]); rs = sb("rs", [128, 1])
        gst = sb("gst", [128, 4, 6]); gmv = sb("gmv", [128, 4, 2]); gve = sb("gve", [128, 4]); grs = sb("grs", [128, 4])

        PA = psum("PA", [128, 1024])
        PZ = [psum("PZ0", [128, 512]), psum("PZ1", [128, 512])]
        PS = psum("PS", [128, 512])
        PT = psum("PT", [128, 512])
        P67 = psum("P67", [128, 1024])
        PTb = PT[:].bitcast(BF16).rearrange("p (k n) -> p k n", k=8)
        P6 = P67[:, 0:512]
        P7 = P67[:, 512:1024]

        def OP(eng, fn, reads=(), writes=(), slot=None, cost=300.0, nbytes=0, store=False, aset=None):
            return dict(eng=eng, fn=fn, reads=list(reads), writes=list(writes), slot=slot, cost=cost,
                        nbytes=nbytes, store=store, aset=aset)

        c_mm = lambda n: 8.0 + 0.405 * n
        C_TR = 64.0
        c_act = lambda n: 200.0 + 0.87 * n
        c_dve = lambda n: 90.0 + 1.05 * n
        c_pool = lambda n: 150.0 + 2.15 * n
        C_POW = 1000.0

        setup = []
        setup.append(OP("sp", lambda e: e.dma_start(out=cf[:], in_=cf_d), writes=[B("cf")], slot="c_cf", nbytes=128 * CF_W * 4))
        setup.append(OP("sp", lambda e: e.dma_start(out=gT[:], in_=gT_d), writes=[B("gT")], slot="c_gT", nbytes=4096))
        setup.append(OP("sp", lambda e: e.dma_start(out=gvec[:], in_=gvec_d), writes=[B("gvec")], slot="c_gvec", nbytes=1 << 20))
        setup.append(OP("sp", lambda e: e.dma_start(out=b40[:], in_=b40_d), writes=[B("b40")], slot="c_b40", nbytes=40960))
        setup.append(OP("pool", lambda e: e.memset(mhalf[:], -0.5), writes=[B("mhalf")], cost=200))

        stg = [shared[:, i * 2048:(i + 1) * 2048] for i in range(4)]
        stg_cnt = [0]

        def stage_cast(src_ap, dst_ap, shape3, dst_buf):
            i = stg_cnt[0] % 4
            stg_cnt[0] += 1
            k, n = shape3
            sview = stg[i][:, 0:k * n].rearrange("p (k n) -> p k n", k=k)
            setup.append(OP("sp", lambda e: e.dma_start(out=sview, in_=src_ap), writes=[B("stg%d" % i)], slot="stg%d" % i,
                            nbytes=128 * k * n * 4))
            if stg_cnt[0] % 2 == 0:
                setup.append(OP("dve", lambda e: e.tensor_copy(out=dst_ap, in_=sview), reads=[B("stg%d" % i)], writes=[dst_buf],
                                cost=90 + 0.53 * k * n))
            else:
                setup.append(OP("act", lambda e: e.activation(out=dst_ap, in_=sview, func=AF.Copy), reads=[B("stg%d" % i)],
                                writes=[dst_buf], cost=c_act(k * n)))

        setup.append(OP("dve", lambda e: e.tensor_copy(out=identb[:], in_=cfs("ident")), reads=[B("cf")], writes=[B("identb")], cost=200))
        setup.append(OP("dve", lambda e: e.tensor_copy(out=b40h[:], in_=b40[:]), reads=[B("b40")], writes=[B("b40h")], cost=200))
        setup.append(OP("dve", lambda e: e.tensor_copy(out=b40f[:], in_=b40h[:]), reads=[B("b40h")], writes=[B("b40f")], cost=200))
        setup.append(OP("dve", lambda e: e.tensor_copy(out=b40l[0:32, :], in_=b40h[0:32, :]), reads=[B("b40h")], writes=[B("b40l")], cost=200))
        setup.append(OP("dve", lambda e: e.tensor_tensor(out=b40l[32:40, :], in0=b40[32:40, :], in1=b40f[32:40, :], op=ALU.subtract),
                        reads=[B("b40"), B("b40f")], writes=[B("b40l")], cost=200))

        w_in_v = w_in.rearrange("(k p) n -> p k n", p=128)

        def load_ws(src, mask_name, dst, dname):
            i = stg_cnt[0] % 4
            stg_cnt[0] += 1
            sview = stg[i][:, 0:1024].rearrange("p (g n) -> p g n", g=8)
            setup.append(OP("sp", lambda e: e.dma_start(out=sview, in_=src), writes=[B("stg%d" % i)], slot="stg%d" % i, nbytes=1 << 19))
            setup.append(OP("dve", lambda e: e.tensor_tensor(out=dst[:], in0=sview,
                                                             in1=cfs(mask_name).unsqueeze(1).to_broadcast([128, 8, 128]), op=ALU.mult),
                            reads=[B("stg%d" % i), B("cf")], writes=[B(dname)], cost=c_dve(1024)))

        first = True
        for nb in ZORDER:
            for kh in range(2):
                stage_cast(w_in_v[:, 4 * kh:4 * kh + 4, nb * 512:(nb + 1) * 512],
                           win[:, 4 * kh:4 * kh + 4, nb * 512:(nb + 1) * 512], (4, 512), B("win%d" % nb))
            if first:
                first = False
                load_ws(wsT_d, "tm", wsT, "wsT")
                i0 = stg_cnt[0] % 4
                stg_cnt[0] += 1
                e40s = stg[i0][0:40, 0:512]
                setup.append(OP("sp", lambda e: e.dma_start(out=e40s, in_=e40_d), writes=[B("stg%d" % i0)], slot="stg%d" % i0, nbytes=81920))
                setup.append(OP("dve", lambda e: e.tensor_copy(out=e40[:], in_=e40s), reads=[B("stg%d" % i0)], writes=[B("e40")], cost=300))
        w_out_v = w_out.rearrange("(k p) n -> p k n", p=128)
        for c in range(4):
            stage_cast(w_out_v[:, 2 * c:2 * c + 2, :], wout[:, 2 * c:2 * c + 2, :], (2, 1024), B("wout"))
        w_gate_v = w_gate.rearrange("(k p) n -> p k n", p=128)
        for c in range(4):
            stage_cast(w_gate_v[:, 2 * c:2 * c + 2, :], wgate[:, 2 * c:2 * c + 2, :], (2, 1024), B("wgate"))
        w_ple_v = w_ple.rearrange("(k p) n -> p k n", p=128)
        stage_cast(w_ple_v, wple[:], (2, 1024), B("wple"))
        load_ws(wsTs_d, "tms", wsTs, "wsTs")

        shb = shared[:].bitcast(BF16)
        S0f = [shared[:, i * 512:(i + 1) * 512] for i in range(8)]
        S0bf = [shb[:, 8192 + j * 512:8192 + (j + 1) * 512] for j in range(3)]
        oTs = shared[:, 4864:5376]
        VM = shb[:, 10752:12800].rearrange("p (b n) -> p b n", b=4)
        CSB = shared[:, 7168:8192]
        SHARED_ALL = [B("stg%d" % i) for i in range(4)]

        def tile_prog(pos, t):
            ops = []
            A = lambda *a, **k: ops.append(OP(*a, **k))
            sample = (t == NCH)
            xi = pos % 3
            r2 = pos % 2
            X = xb[xi]; BX = B("x%d" % xi)
            Pt = pb[r2]; BP = B("p%d" % r2)
            RP = rb[r2]; BR = B("rope%d" % r2)
            pT = pTb[xi]; BPT = B("pT%d" % xi)
            xT = xTb[r2]; BXT = B("xT%d" % r2)
            Qr = Qrb[r2]; Kr = Krb[r2]; Vb = Vbb[r2]
            ssq = ssqb[r2]; ms = msb[r2]; rstd = rstdb[r2]
            NSSQ, NMS, NRSTD = "ssq_%d" % r2, "ms_%d" % r2, "rstd_%d" % r2
            NQ, NK, NV = "Qr%d" % r2, "Kr%d" % r2, "Vb%d" % r2
            cos2 = RP[:, 0:128]
            sin2 = RP[:, 128:256]
            if not sample:
                A("sp", lambda e: e.dma_start(out=X[:], in_=xp[t * 128:(t + 1) * 128, :]), writes=[BX], slot="x%d" % xi, nbytes=1 << 19, cost=100)
                A("sp", lambda e: e.dma_start(out=Pt[:], in_=pp[t * 128:(t + 1) * 128, :]), writes=[BP], slot="p%d" % r2, nbytes=1 << 17, cost=100)
            else:
                A("sp", lambda e: e.dma_start(out=X[:], in_=xs), writes=[BX], slot="x%d" % xi, nbytes=1 << 19, cost=100)
                A("sp", lambda e: e.dma_start(out=Pt[:], in_=psm), writes=[BP], slot="p%d" % r2, nbytes=1 << 17, cost=100)
            A("sp", lambda e: e.dma_start(out=RP[:], in_=rope_d[t]), writes=[BR], slot="rope%d" % r2, nbytes=1 << 17, cost=100)
            if sample:
                CS = CSB; BCS = B("csb")
                A("sp", lambda e: e.dma_start(out=CS, in_=cs_d), writes=[BCS] + SHARED_ALL, slot="c_cs", nbytes=1 << 19, cost=100)
                WQT = CS[:, 0:512].rearrange("p (h n) -> p h n", h=4)
                DD = CS[:, 512:1024].rearrange("p (h n) -> p h n", h=4)
                BWQ = BCS
                WKV = cfs("wkvs")
                WS = wsTs; BWS = B("wsTs"); bcol = 128
            else:
                WQT = cfs("wqt").rearrange("p (h n) -> p h n", h=4)
                DD = cfs("dd").rearrange("p (h n) -> p h n", h=4)
                BWQ = B("cf")
                WKV = cfs("wkv")
                WS = wsT; BWS = B("wsT"); bcol = 0

            A("act", lambda e: e.activation(out=xT[:].rearrange("p k n -> p (k n)"), in_=X[:], func=AF.Square, accum_out=ssq[:]),
              reads=[BX], writes=[BXT, B(NSSQ)], cost=c_act(1024) + 100)
            A("pool", lambda e: e.tensor_scalar(out=ms[:], in0=ssq[:], scalar1=1.0 / D, scalar2=RMS_EPS, op0=ALU.mult, op1=ALU.add),
              reads=[B(NSSQ)], writes=[B(NMS)], cost=200)
            A("pool", lambda e: e.tensor_tensor(out=rstd[:], in0=ms[:], in1=mhalf[:, 0:1], op=ALU.pow),
              reads=[B(NMS), B("mhalf")], writes=[B(NRSTD)], cost=550)
            PAv = PA[:].rearrange("p (k n) -> p k n", k=8)
            for k in range(8):
                A("pe", lambda e, k=k: e.transpose(out=PAv[:, k, :], in_=X[:, k * 128:(k + 1) * 128], identity=cfs("ident")),
                  reads=[BX, B("cf")], writes=[B("PA")], cost=C_TR)
            A("dve", lambda e: e.tensor_tensor(out=xT[:], in0=PAv, in1=gT[:].unsqueeze(2).to_broadcast([128, 8, 128]), op=ALU.mult),
              reads=[B("PA"), B("gT")], writes=[BXT], cost=c_dve(1024))
            PTf = PT[:, 0:256].rearrange("p (k n) -> p k n", k=2)
            for k in range(2):
                A("pe", lambda e, k=k: e.transpose(out=PTf[:, k, :], in_=Pt[:, k * 128:(k + 1) * 128], identity=cfs("ident")),
                  reads=[BP, B("cf")], writes=[B("PT")], cost=C_TR)
            A("dve", lambda e: e.tensor_copy(out=pT[:], in_=PTf), reads=[B("PT")], writes=[BPT], cost=c_dve(256))

            def zblock(zi, nb):
                zb = PZ[zi % 2][:]
                BZ = B("PZ%d" % (zi % 2))
                for k in range(8):
                    A("pe", lambda e, k=k: e.matmul(out=zb, lhsT=xT[:, k, :], rhs=win[:, k, nb * 512:(nb + 1) * 512],
                                                    start=(k == 0), stop=(k == 7)),
                      reads=[BXT, B("win%d" % nb)], writes=[BZ], cost=c_mm(512))
                return zb, BZ

            for zi, nb in enumerate(ZORDER):
                zb, BZ = zblock(zi, nb)
                zb3 = zb.rearrange("p (h n) -> p h n", h=4)
                if nb == 1:
                    A("act", lambda e, zb=zb: e.activation(out=sv[:], in_=zb, func=AF.Gelu_apprx_tanh, scale=rstd[:]),
                      reads=[BZ, B(NRSTD)], writes=[B("sv")], cost=c_act(512) + 100, aset="gelu")
                    A("dve", lambda e: e.bn_stats(out=st6[:], in_=sv[:]), reads=[B("sv")], writes=[B("st6")], cost=c_dve(512))
                    A("dve", lambda e: e.bn_aggr(out=mv[:], in_=st6[:]), reads=[B("st6")], writes=[B("mv")], cost=200)
                    A("pool", lambda e: e.tensor_scalar(out=ve[:], in0=mv[:, 1:2], scalar1=LN_EPS, scalar2=None, op0=ALU.add),
                      reads=[B("mv")], writes=[B("ve")], cost=200)
                    A("pool", lambda e: e.tensor_tensor(out=rs[:], in0=ve[:], in1=mhalf[:, 0:1], op=ALU.pow),
                      reads=[B("ve"), B("mhalf")], writes=[B("rs")], cost=550)
                    A("dve", lambda e: e.tensor_scalar(out=sv[:], in0=sv[:], scalar1=mv[:, 0:1], scalar2=rs[:], op0=ALU.subtract, op1=ALU.mult),
                      reads=[B("sv"), B("mv"), B("rs")], writes=[B("sv")], cost=c_dve(512))
                    if not sample:
                        A("dve", lambda e: e.tensor_tensor(out=vbf[:], in0=sv[:], in1=gvec[:, 1024:1536], op=ALU.mult),
                          reads=[B("sv"), B("gvec")], writes=[B("vbf")], cost=c_dve(512))
                    else:
                        A("pool", lambda e: e.tensor_tensor(out=vf, in0=sv[:], in1=gvec[:, 1024:1536], op=ALU.mult),
                          reads=[B("sv"), B("gvec")], writes=[B("vf")], cost=c_pool(512))
                        A("sp", lambda e: e.dma_start(out=v_out, in_=vf), reads=[B("vf")], writes=[B("v_out")], slot="o_v",
                          nbytes=1 << 18, cost=100, store=True)
                        A("pool", lambda e: e.tensor_copy(out=vbf[:], in_=vf), reads=[B("vf")], writes=[B("vbf")], cost=1900)
                elif nb == 0:
                    A("act", lambda e, zb=zb: e.activation(out=su[:], in_=zb, func=AF.Gelu_apprx_tanh, scale=rstd[:]),
                      reads=[BZ, B(NRSTD)], writes=[B("su")], cost=c_act(512) + 100, aset="gelu")
                elif nb == 2:
                    A("act", lambda e, zb=zb: e.activation(out=sgs[:], in_=zb, func=AF.Silu, scale=rstd[:]),
                      reads=[BZ, B(NRSTD)], writes=[B("sgs")], cost=c_act(512) + 100, aset="silu")
                    A("pool", lambda e: e.tensor_tensor(out=tsg[:], in0=su[:], in1=sgs[:], op=ALU.mult),
                      reads=[B("su"), B("sgs")], writes=[B("tsg")], cost=c_pool(512))
                elif nb in (3, 4):
                    dst = Qr if nb == 3 else Kr
                    BD = B(NQ) if nb == 3 else B(NK)
                    A("dve", lambda e, zb3=zb3: e.scalar_tensor_tensor(out=rtA[:], in0=zb3, scalar=rstd[:],
                                                                       in1=cos2.unsqueeze(1).to_broadcast([128, 4, 128]),
                                                                       op0=ALU.mult, op1=ALU.mult),
                      reads=[BZ, B(NRSTD), BR], writes=[B("rtA")], cost=c_dve(512))
                    A("dve", lambda e, zb3=zb3: e.scalar_tensor_tensor(out=rtB[:, :, 0:64], in0=zb3[:, :, 64:128], scalar=rstd[:],
                                                                       in1=sin2[:, 0:64].unsqueeze(1).to_broadcast([128, 4, 64]),
                                                                       op0=ALU.mult, op1=ALU.mult),
                      reads=[BZ, B(NRSTD), BR], writes=[B("rtB0")], cost=c_dve(256))
                    A("dve", lambda e, zb3=zb3: e.scalar_tensor_tensor(out=rtB[:, :, 64:128], in0=zb3[:, :, 0:64], scalar=rstd[:],
                                                                       in1=sin2[:, 64:128].unsqueeze(1).to_broadcast([128, 4, 64]),
                                                                       op0=ALU.mult, op1=ALU.mult),
                      reads=[BZ, B(NRSTD), BR], writes=[B("rtB1")], cost=c_dve(256))
                    A("pool", lambda e, dst=dst: e.tensor_tensor(out=dst[:], in0=rtA[:], in1=rtB[:], op=ALU.add),
                      reads=[B("rtA"), B("rtB")], writes=[BD], cost=c_pool(512))
                elif nb == 5:
                    A("act", lambda e, zb3=zb3: e.activation(out=Vb[:], in_=zb3, func=AF.Identity, scale=rstd[:]),
                      reads=[BZ, B(NRSTD)], writes=[B(NV)], cost=c_act(512) + 100)
                    A("pool", lambda e: e.tensor_tensor(out=Vh[:], in0=Vb[:], in1=WKV.unsqueeze(2).to_broadcast([128, 4, 128]), op=ALU.mult),
                      reads=[B(NV), B("cf")], writes=[B("Vh")], cost=1050)
                elif nb == 6:
                    A("act", lambda e, zb=zb: e.activation(out=rgs[:], in_=zb, func=AF.Silu, scale=rstd[:]),
                      reads=[BZ, B(NRSTD)], writes=[B("rgs")], cost=c_act(512) + 100, aset="silu")
                    A("pool", lambda e: e.tensor_tensor(out=G2[:], in0=rgs[:], in1=gvec[:, 1536:2048], op=ALU.mult),
                      reads=[B("rgs"), B("gvec")], writes=[B("G2")], cost=c_pool(512))
                if nb == 2:
                    A("pe", lambda e: e.matmul(out=PS[:], lhsT=b40l[:, bcol:bcol + 128], rhs=e40[:], start=True, stop=False),
                      reads=[B("b40l"), B("e40")], writes=[B("PS")], cost=c_mm(512))
                    for g in range(8):
                        A("pe", lambda e, g=g: e.matmul(out=PS[:, g * 64:(g + 1) * 64], lhsT=WS[:, g, :], rhs=vbf[:, g * 64:(g + 1) * 64],
                                                        start=False, stop=(g == 7)),
                          reads=[BWS, B("vbf")], writes=[B("PS")], cost=c_mm(64))
                    A("dve", lambda e: e.tensor_tensor(out=mix[:, 0:512], in0=tsg[:], in1=PS[:], op=ALU.mult),
                      reads=[B("tsg"), B("PS")], writes=[B("mixA")], cost=c_dve(512))

            for h in range(4):
                A("pe", lambda e, h=h: e.transpose(out=PTb[:, h, :], in_=Qr[:, h, :], identity=identb[:]),
                  reads=[B(NQ), B("identb")], writes=[B("PT")], cost=C_TR)
            for h in range(4):
                A("pe", lambda e, h=h: e.transpose(out=PTb[:, 4 + h, :], in_=Kr[:, h, :], identity=identb[:]),
                  reads=[B(NK), B("identb")], writes=[B("PT")], cost=C_TR)
            A("dve", lambda e: e.tensor_tensor(out=QKT[:, 0:4, :], in0=PTb[:, 0:4, :], in1=WQT, op=ALU.mult),
              reads=[B("PT"), BWQ], writes=[B("QT")], cost=c_dve(512))
            A("act", lambda e: e.activation(out=QKT[:, 4:8, :], in_=PTb[:, 4:8, :], func=AF.Copy),
              reads=[B("PT")], writes=[B("KT")], cost=c_act(512))
            PSv = PS[:].rearrange("p (h n) -> p h n", h=4)
            P6v = P6.rearrange("p (h n) -> p h n", h=4)
            P7v = P7.rearrange("p (h n) -> p h n", h=4)
            for h in range(4):
                A("pe", lambda e, h=h: e.matmul(out=PSv[:, h, :], lhsT=QKT[:, 4 + h, :], rhs=QKT[:, h, :], start=True, stop=True),
                  reads=[B("QT"), B("KT")], writes=[B("PS")], cost=c_mm(128))
            A("dve", lambda e: e.tensor_tensor(out=scT[:], in0=PSv, in1=DD, op=ALU.mult),
              reads=[B("PS"), BWQ], writes=[B("scT")], cost=c_dve(512))

            if not sample:
                for h in range(4):
                    A("pe", lambda e, h=h: e.matmul(out=P6v[:, h, :], lhsT=Kr[:, h, :], rhs=Vh[:, h, :], start=True, stop=True),
                      reads=[B(NK), B("Vh")], writes=[B("P6")], cost=c_mm(128))
                for h in range(4):
                    A("pe", lambda e, h=h: e.matmul(out=P7v[:, h, :], lhsT=scT[:, h, :], rhs=Vb[:, h, :], start=True, stop=(t == 0)),
                      reads=[B("scT"), B(NV)], writes=[B("P7")], cost=c_mm(128))
                    if t > 0:
                        A("pe", lambda e, h=h: e.matmul(out=P7v[:, h, :], lhsT=QKT[:, h, :], rhs=Sbf[:, h, :], start=False, stop=True),
                          reads=[B("QT"), B("Sbf")], writes=[B("P7")], cost=c_mm(128))
                if t == 0:
                    A("dve", lambda e: e.tensor_copy(out=Sst[:], in_=P6v), reads=[B("P6")], writes=[B("Sst")], cost=c_dve(512))
                else:
                    for h in range(4):
                        A("dve", lambda e, h=h: e.scalar_tensor_tensor(out=Sst[:, h, :], in0=Sst[:, h, :], scalar=float(GAM[h] ** 128),
                                                                       in1=P6v[:, h, :], op0=ALU.mult, op1=ALU.add),
                          reads=[B("P6"), B("Sst%d" % h)], writes=[B("Sst%d" % h)], cost=300)
                if t < nch - 1:
                    A("act", lambda e: e.activation(out=Sbf[:], in_=Sst[:], func=AF.Copy), reads=[B("Sst")], writes=[B("Sbf")], cost=c_act(512))
                else:
                    A("sp", lambda e: e.dma_start(out=sp_out.rearrange("h d e -> d h e"), in_=Sst[:]),
                      reads=[B("Sst")], writes=[B("sp_out")], slot="o_sp", nbytes=1 << 18, cost=100, store=True)
            else:
                G8 = cfs("g8")
                BMc = cfs("bm")
                for b in range(16):
                    ks = b % 8
                    js = b % 3
                    S0 = S0f[ks]
                    BS0 = B("S0f%d" % ks)
                    S0b = S0bf[js]
                    BS0b = B("S0bf%d" % js)
                    S0v = S0.rearrange("p (h n) -> p h n", h=4)
                    A("sp", lambda e, b=b, S0v=S0v: e.dma_start(out=S0v, in_=st_in[b].rearrange("h d e -> d h e")),
                      writes=[BS0] + (SHARED_ALL if b < 8 else []), slot="s0_%d" % ks, nbytes=1 << 18, cost=100)
                    A("act", lambda e, S0=S0, S0b=S0b: e.activation(out=S0b, in_=S0, func=AF.Copy),
                      reads=[BS0], writes=[BS0b] + (SHARED_ALL if b < 3 else []), cost=c_act(512))
                    if b % 4 == 0:
                        for bb in range(4):
                            A("dve", lambda e, b=b, bb=bb: e.tensor_scalar(out=VM[:, bb, :], in0=Vh[:].rearrange("p h n -> p (h n)"),
                                                                           scalar1=BMc[:, b + bb:b + bb + 1], scalar2=None, op0=ALU.mult),
                              reads=[B("Vh"), B("cf")], writes=[B("VM%d" % bb)] + (SHARED_ALL if b == 0 else []), cost=300)
                    PU = PZ[b % 2][:]
                    BPU = B("PZ%d" % (b % 2))
                    PUv = PU.rearrange("p (h n) -> p h n", h=4)
                    VMv = VM[:, b % 4, :].rearrange("p (h n) -> p h n", h=4)
                    for h in range(4):
                        A("pe", lambda e, h=h, PUv=PUv, VMv=VMv: e.matmul(out=PUv[:, h, :], lhsT=Kr[:, h, :], rhs=VMv[:, h, :], start=True, stop=True),
                          reads=[B(NK), B("VM%d" % (b % 4))], writes=[BPU], cost=c_mm(128))
                    for h in range(4):
                        A("pe", lambda e, h=h, b=b, S0b=S0b: e.matmul(out=P6[:, h * 128 + 8 * b:h * 128 + 8 * b + 8],
                                                                      lhsT=S0b[:, h * 128:(h + 1) * 128], rhs=QKT[:, h, 8 * b:8 * b + 8],
                                                                      start=True, stop=True),
                          reads=[BS0b, B("QT")], writes=[B("P6")], cost=70)
                    A("pool", lambda e, S0=S0: e.tensor_tensor(out=S0, in0=S0, in1=G8, op=ALU.mult),
                      reads=[BS0, B("cf")], writes=[BS0], cost=c_pool(512))
                    A("dve", lambda e, S0=S0, PU=PU: e.tensor_tensor(out=S0, in0=S0, in1=PU, op=ALU.add),
                      reads=[BS0, BPU], writes=[BS0], cost=c_dve(512))
                    A("sp", lambda e, b=b, S0v=S0v: e.dma_start(out=ss_out[b].rearrange("h d e -> d h e"), in_=S0v),
                      reads=[BS0], writes=[B("ss_out")], slot="o_ss%d" % ks, nbytes=1 << 18, cost=100, store=True)
                A("act", lambda e: e.activation(out=oTs, in_=P6, func=AF.Copy), reads=[B("P6")], writes=[B("oTs")] + SHARED_ALL, cost=c_act(512))
                for h in range(4):
                    A("pe", lambda e, h=h: e.matmul(out=P7v[:, h, :], lhsT=scT[:, h, :], rhs=Vb[:, h, :], start=True, stop=False),
                      reads=[B("scT"), B(NV)], writes=[B("P7")], cost=c_mm(128))
                    A("pe", lambda e, h=h: e.matmul(out=P7v[:, h, :], lhsT=oTs[:, h * 128:(h + 1) * 128], rhs=cfs("ident"), start=False, stop=True),
                      reads=[B("oTs"), B("cf")], writes=[B("P7")], cost=4 * c_mm(128))

            for h in range(4):
                A("dve", lambda e, h=h: e.bn_stats(out=gst[:, h, :], in_=P7v[:, h, :]), reads=[B("P7")], writes=[B("gst%d" % h)], cost=210)
            for h in range(4):
                A("dve", lambda e, h=h: e.bn_aggr(out=gmv[:, h, :], in_=gst[:, h, :]), reads=[B("gst%d" % h)], writes=[B("gmv%d" % h)], cost=100)
            A("pool", lambda e: e.tensor_scalar(out=gve[:], in0=gmv[:, :, 1], scalar1=LN_EPS, scalar2=None, op0=ALU.add),
              reads=[B("gmv")], writes=[B("gve")], cost=200)
            A("pool", lambda e: e.tensor_tensor(out=grs[:], in0=gve[:], in1=mhalf[:], op=ALU.pow),
              reads=[B("gve"), B("mhalf")], writes=[B("grs")], cost=C_POW)
            for h in range(4):
                A("dve", lambda e, h=h: e.tensor_scalar(out=ot[:, h, :], in0=P7v[:, h, :], scalar1=gmv[:, h, 0:1], scalar2=grs[:, h:h + 1],
                                                        op0=ALU.subtract, op1=ALU.mult),
                  reads=[B("P7"), B("gmv%d" % h), B("grs")], writes=[B("ot%d" % h)], cost=340)
            A("pool", lambda e: e.tensor_tensor(out=mix[:, 512:1024], in0=ot[:].rearrange("p h n -> p (h n)"), in1=G2[:], op=ALU.mult),
              reads=[B("ot"), B("G2")], writes=[B("mixB")], cost=c_pool(512))

            for k in range(8):
                A("pe", lambda e, k=k: e.transpose(out=PTb[:, k, :], in_=mix[:, k * 128:(k + 1) * 128], identity=identb[:]),
                  reads=[B("mixA"), B("mixB"), B("identb")], writes=[B("PT")], cost=C_TR)
            A("act", lambda e: e.activation(out=mixT[:], in_=PTb, func=AF.Copy), reads=[B("PT")], writes=[B("mixT")], cost=c_act(1024))
            for n in range(2):
                for k in range(8):
                    A("pe", lambda e, k=k, n=n: e.matmul(out=P67[:, n * 512:(n + 1) * 512], lhsT=mixT[:, k, :], rhs=wout[:, k, n * 512:(n + 1) * 512],
                                                         start=(k == 0), stop=(k == 7)),
                      reads=[B("mixT"), B("wout")], writes=[B("P6") if n == 0 else B("P7")], cost=c_mm(512))
            A("act", lambda e: e.activation(out=bufA[:], in_=P67[:], func=AF.Square, accum_out=ssq2[:]),
              reads=[B("P67")], writes=[B("bufA"), B("ssq2")], cost=c_act(1024) + 100)
            A("pool", lambda e: e.tensor_scalar(out=ms2[:], in0=ssq2[:], scalar1=1.0 / D, scalar2=RMS_EPS, op0=ALU.mult, op1=ALU.add),
              reads=[B("ssq2")], writes=[B("ms2")], cost=200)
            A("pool", lambda e: e.tensor_tensor(out=rstd2[:], in0=ms2[:], in1=mhalf[:, 0:1], op=ALU.pow),
              reads=[B("ms2"), B("mhalf")], writes=[B("rstd2")], cost=550)
            A("dve", lambda e: e.scalar_tensor_tensor(out=bufA[:], in0=P67[:], scalar=rstd2[:], in1=gvec[:, 0:1024], op0=ALU.mult, op1=ALU.mult),
              reads=[B("P67"), B("rstd2"), B("gvec")], writes=[B("bufA")], cost=c_dve(1024))
            A("dve", lambda e: e.tensor_tensor(out=X[:], in0=X[:], in1=bufA[:], op=ALU.add),
              reads=[BX, B("bufA")], writes=[BX], cost=c_dve(1024))
            for k in range(8):
                A("pe", lambda e, k=k: e.transpose(out=PAv[:, k, :], in_=X[:, k * 128:(k + 1) * 128], identity=cfs("ident")),
                  reads=[BX, B("cf")], writes=[B("PA")], cost=C_TR)
            A("act", lambda e: e.activation(out=x1T[:], in_=PAv, func=AF.Copy), reads=[B("PA")], writes=[B("x1T")], cost=c_act(1024))
            for n in range(2):
                for k in range(8):
                    A("pe", lambda e, k=k, n=n: e.matmul(out=P67[:, n * 512:(n + 1) * 512], lhsT=x1T[:, k, :], rhs=wgate[:, k, n * 512:(n + 1) * 512],
                                                         start=(k == 0), stop=(k == 7)),
                      reads=[B("x1T"), B("wgate")], writes=[B("P6") if n == 0 else B("P7")], cost=c_mm(512))
            for n in range(2):
                for k in range(2):
                    A("pe", lambda e, k=k, n=n: e.matmul(out=PA[:, n * 512:(n + 1) * 512], lhsT=pT[:, k, :], rhs=wple[:, k, n * 512:(n + 1) * 512],
                                                         start=(k == 0), stop=(k == 1)),
                      reads=[BPT, B("wple")], writes=[B("PA")], cost=c_mm(512))
            A("act", lambda e: e.activation(out=bufA[:], in_=P67[:], func=AF.Tanh, scale=0.5), reads=[B("P67")], writes=[B("bufA")], cost=c_act(1024))
            A("dve", lambda e: e.scalar_tensor_tensor(out=bufA[:], in0=bufA[:], scalar=1.0, in1=PA[:], op0=ALU.add, op1=ALU.mult),
              reads=[B("bufA"), B("PA")], writes=[B("bufA")], cost=c_dve(1024))
            A("dve", lambda e: e.scalar_tensor_tensor(out=X[:], in0=bufA[:], scalar=0.5, in1=X[:], op0=ALU.mult, op1=ALU.add),
              reads=[B("bufA"), BX], writes=[BX], cost=c_dve(1024))
            dsty = ys if sample else yp[t * 128:(t + 1) * 128, :]
            A("sp", lambda e: e.dma_start(out=dsty, in_=X[:]), reads=[BX], writes=[B("y_out%d" % xi)], slot="oy%d" % xi,
              nbytes=1 << 19, cost=100, store=True)
            return ops

        tiles = list(range(nch))
        if with_sample:
            tiles.insert(min(SAMPLE_POS, nch), NCH)
        progs = [setup] + [tile_prog(i, t) for i, t in enumerate(tiles)]

        def _flatb(xs):
            out = []
            for x in xs:
                if isinstance(x, (list, tuple)):
                    out.extend(_flatb(x))
                else:
                    out.append(x)
            return out

        for prog in progs:
            for d in prog:
                d["reads"] = _flatb(d["reads"])
                d["writes"] = _flatb(d["writes"])
                d["rset"] = set(id(b) for b in d["reads"])
                d["wset"] = set(id(b) for b in d["writes"])

        npred = []
        succs = []
        for prog in progs:
            lastw = {}
            readers = {}
            np_ = [0] * len(prog)
            sc = [[] for _ in prog]
            for i, d in enumerate(prog):
                ps = set()
                for bid in d["rset"]:
                    if bid in lastw:
                        ps.add(lastw[bid])
                for bid in d["wset"]:
                    if bid in lastw:
                        ps.add(lastw[bid])
                    for j in readers.get(bid, ()):
                        ps.add(j)
                ps.discard(i)
                if prog is setup and i > 0:
                    ps.add(i - 1)
                for j in ps:
                    sc[j].append(i)
                np_[i] = len(ps)
                for bid in d["wset"]:
                    lastw[bid] = i
                    readers[bid] = []
                for bid in d["rset"]:
                    if bid not in d["wset"]:
                        readers.setdefault(bid, []).append(i)
            npred.append(np_)
            succs.append(sc)
        ready = [sorted(i for i in range(len(p)) if npred[k][i] == 0) for k, p in enumerate(progs)]
        left = [len(p) for p in progs]

        pending = {}
        for d in setup:
            for bid in d["wset"]:
                pending[bid] = pending.get(bid, 0) + 1

        FLOW = set(id(b) for b in _flatb([B("Sst"), B("Sbf")]))
        tile_written = set()
        for pi in range(1, len(progs)):
            for d in progs[pi]:
                tile_written |= d["wset"]
        seglen = {}
        segops = {}
        flow_left = {}
        for pi in range(1, len(progs)):
            acc = {}
            for oi, d in enumerate(progs[pi]):
                for bid in d["rset"] | d["wset"]:
                    if bid in tile_written:
                        acc.setdefault(bid, []).append((oi, bid in d["rset"], bid in d["wset"]))
            for bid, lst in acc.items():
                if bid in FLOW:
                    flow_left.setdefault(bid, {})[pi] = len(lst)
                    continue
                st0 = 0
                had_read = False
                for k, (oi, isr, isw) in enumerate(lst):
                    fresh = isw and not isr
                    if k > 0 and fresh and had_read:
                        for kk in range(st0, k):
                            seglen[(pi, lst[kk][0], bid)] = k - st0
                            segops[(pi, lst[kk][0], bid)] = [x[0] for x in lst[st0:k]]
                        st0 = k
                        had_read = False
                    if isr:
                        had_read = True
                for kk in range(st0, len(lst)):
                    seglen[(pi, lst[kk][0], bid)] = len(lst) - st0
                    segops[(pi, lst[kk][0], bid)] = [x[0] for x in lst[st0:]]
        holder = {}

        bank_ids = set(id(bufs[n]) for n in PSUM_BANKS if n in bufs)

        def eligible(pi, oi, d):
            for bid in d["rset"] | d["wset"]:
                if pending.get(bid, 0) > 0:
                    return False
                if bid not in tile_written:
                    continue
                if bid in FLOW:
                    for pj, lf in flow_left[bid].items():
                        if pj < pi and lf > 0:
                            return False
                else:
                    h = holder.get(bid)
                    if h is not None and h[0] != pi:
                        return False
                    if h is None and bid in bank_ids:
                        for oj in segops[(pi, oi, bid)]:
                            dj = progs[pi][oj]
                            for b2 in dj["rset"] | dj["wset"]:
                                h2 = holder.get(b2)
                                if h2 is not None and h2[0] != pi:
                                    return False
            return True

        eng_free = {e: 0.0 for e in COMPUTE + ("sp",)}
        dma_free = [0.0]
        act_set = [None]
        store_events = []
        active = [0] + list(range(1, min(len(progs), 1 + window)))
        nxt = 1 + window
        while active:
            best = None
            best_key = None
            for pi in active:
                for oi in ready[pi]:
                    d = progs[pi][oi]
                    if pi != 0 and not eligible(pi, oi, d):
                        continue
                    deps = S.peek(d["eng"], d["reads"], d["writes"], d["slot"])
                    t_ready = max([ev.t_done for ev in deps], default=0.0)
                    start = max(t_ready, eng_free[d["eng"]])
                    key = (start + AGE_BIAS * (pi - active[1 if (len(active) > 1 and active[0] == 0) else 0]), pi, oi)
                    if best is None or key[0] < best_key[0] - TIE or (abs(key[0] - best_key[0]) <= TIE and key[1:] < best_key[1:]):
                        best, best_key, best_start = (pi, oi), key, start
                    if pi == 0:
                        break
            if best is None:
                names = {id(b): n for n, b in bufs.items()}
                msg = []
                for pi in active:
                    for oi in ready[pi][:6]:
                        d = progs[pi][oi]
                        why = []
                        for bid in d["rset"] | d["wset"]:
                            if pending.get(bid, 0) > 0:
                                why.append("pending:" + names.get(bid, "?"))
                            h = holder.get(bid)
                            if h is not None and h[0] != pi:
                                why.append("held:%s by %d (%d left)" % (names.get(bid, "?"), h[0], h[1]))
                            if bid in FLOW:
                                why.append("flow:" + names.get(bid, "?") + str(flow_left[bid]))
                        msg.append("tile %d op %d %s: %s" % (pi, oi, d["eng"], why))
                raise RuntimeError("list scheduler deadlock\n" + "\n".join(msg))
            pi, oi = best
            d = progs[pi][oi]
            if pi == 0:
                for bid in d["wset"]:
                    pending[bid] -= 1
            else:
                for bid in d["rset"] | d["wset"]:
                    if bid not in tile_written:
                        continue
                    if bid in FLOW:
                        flow_left[bid][pi] -= 1
                    else:
                        h = holder.get(bid)
                        if h is None:
                            h = holder[bid] = [pi, seglen[(pi, oi, bid)]]
                        h[1] -= 1
                        if h[1] == 0:
                            del holder[bid]
            cost = d["cost"]
            if d["eng"] == "pe" and best_start > eng_free["pe"] + 40.0:
                cost = cost + PE_STALL_PEN
            if d.get("aset") is not None and d["eng"] == "act":
                if act_set[0] is not None and d["aset"] != act_set[0]:
                    cost = cost + 1300.0
                act_set[0] = d["aset"]
            if TRACE_SCHED is not None:
                TRACE_SCHED.append((pi, oi, d["eng"], best_start, cost, eng_free[d["eng"]], None, None))
            ev = S.add(d["eng"], d["fn"], d["reads"], d["writes"], d["slot"])
            if d["slot"] is not None:
                eng_free[d["eng"]] = best_start + cost
                xfer = d["nbytes"] / DMA_BW
                t0 = max(best_start, dma_free[0])
                dma_free[0] = t0 + xfer
                ev.t_done = t0 + xfer + DMA_LAT
            else:
                eng_free[d["eng"]] = best_start + cost
                ev.t_done = best_start + cost * SLACK + SYNC_LAT
            if d["store"]:
                store_events.append(ev)
            ready[pi].remove(oi)
            left[pi] -= 1
            for j in succs[pi][oi]:
                npred[pi][j] -= 1
                if npred[pi][j] == 0:
                    ready[pi].append(j)
            ready[pi].sort()
            if left[pi] == 0:
                active.remove(pi)
                if nxt < len(progs):
                    active.append(nxt)
                    nxt += 1
        build_program.est_ns = max(eng_free.values())

        S.finalize()
        S.tail = ("sp", store_events)
        with nc.Block() as block:
            S.emit(block)
    return nc


_PROG = {}


def _get_program():
    if "nc" not in _PROG:
        _PROG["nc"] = build_program()
    return _PROG["nc"]


def _prep(x_prompt, x_sample, state_ret, p_prompt, p_sample, w_in, w_out, norm_pre, norm_post,
          sgu_w, sgu_b, sgu_ln, ret_gn, w_ple_proj, w_ple_gate):
    f = lambda a: np.ascontiguousarray(np.asarray(a, dtype=np.float32))
    x_prompt, x_sample, state_ret = f(x_prompt), f(x_sample), f(state_ret)
    p_prompt, p_sample = f(p_prompt), f(p_sample)
    w_in, w_out, w_ple_proj, w_ple_gate = f(w_in)[0], f(w_out)[0], f(w_ple_proj)[0], f(w_ple_gate)[0]
    norm_pre, norm_post, sgu_w, sgu_b = f(norm_pre)[0], f(norm_post)[0], f(sgu_w)[0], f(sgu_b)[0]
    sgu_ln, ret_gn = f(sgu_ln)[0], f(ret_gn)[0]

    cf, cs, rope_all, e40 = _host_consts()
    gT = np.ascontiguousarray(norm_pre.reshape(8, 128).T)
    gvec = np.ascontiguousarray(np.broadcast_to(np.concatenate([norm_post, sgu_ln, ret_gn])[None, :], (128, 2048)))
    wsT = np.ascontiguousarray(sgu_w.transpose(2, 0, 1))
    wsTs = np.zeros((128, 8, 128), np.float32)
    blk = np.ascontiguousarray(sgu_w[:, :8, :8].transpose(2, 0, 1))
    for b in range(16):
        wsTs[8 * b:8 * b + 8, :, 8 * b:8 * b + 8] = blk
    b40 = np.zeros((40, 256), np.float32)
    b40[0:8, 0:128] = sgu_b
    b40[32:40, 0:128] = sgu_b
    bs = np.tile(sgu_b[:, :8], (1, 16))
    b40[0:8, 128:256] = bs
    b40[32:40, 128:256] = bs

    in_maps = []
    for c in range(NCORES):
        in_maps.append({
            "xp": x_prompt[c], "xs": x_sample[16 * c:16 * c + 16].reshape(128, D),
            "pp": p_prompt[0, c], "psm": p_sample[0, 16 * c:16 * c + 16].reshape(128, PLE),
            "st": state_ret[0, 16 * c:16 * c + 16],
            "w_in": w_in, "w_out": w_out, "w_gate": w_ple_gate, "w_ple": w_ple_proj,
            "gT": gT, "gvec": gvec, "wsT": wsT, "wsTs": wsTs, "b40": b40,
            "cf": cf, "cs": cs, "rope": rope_all, "e40": e40,
        })
    return in_maps


def kernel(**inputs):
    in_maps = _prep(**inputs)
    nc = _get_program()
    res = run_bass_kernel_spmd(nc, in_maps, core_ids=list(range(NCORES)))
    outs = res.results
    y_prompt = np.stack([outs[c]["yp"] for c in range(NCORES)], axis=0)
    y_sample = np.concatenate([outs[c]["ys"].reshape(16, 8, D) for c in range(NCORES)], axis=0)
    st_p = np.stack([outs[c]["sp_out"] for c in range(NCORES)], axis=0)[None]
    st_s = np.concatenate([outs[c]["ss_out"] for c in range(NCORES)], axis=0)[None]
    v_s = np.concatenate([outs[c]["v_out"].reshape(16, 8, 512) for c in range(NCORES)], axis=0)[None]
    return (y_prompt.astype(np.float32), y_sample.astype(np.float32), st_p.astype(np.float32),
            st_s.astype(np.float32), v_s.astype(np.float32))
```

```python
import numpy as np
from contextlib import ExitStack
import concourse.bass as bass
import concourse.mybir as mybir
from concourse.bass_utils import run_bass_kernel_spmd

F32 = mybir.dt.float32
BF16 = mybir.dt.bfloat16
ALU = mybir.AluOpType
AF = mybir.ActivationFunctionType

NCORES = 8
D = 1024
SEQ = 2048
NCH = 16
INW = 3584
PLE = 256
PAST = 16384
RMS_EPS = 1e-6
LN_EPS = 1e-5
GAM = [1.0 - 2.0 ** (-5.0 - h) for h in range(4)]

COMPUTE = ("pe", "act", "dve", "pool")
PSUM_BANKS = ("P01", "P2", "P3", "P4", "P5", "P6", "P7")


class Buf:
    __slots__ = ("name", "w", "r")

    def __init__(self, name):
        self.name = name
        self.w = None
        self.r = []


class Ev:
    __slots__ = ("eng", "idx", "sem", "val", "clock", "needed", "is_dma", "t_done")

    def __init__(self):
        self.eng = None
        self.idx = 0
        self.sem = None
        self.val = 0
        self.clock = None
        self.needed = False
        self.is_dma = False
        self.t_done = 0.0

    def key(self):
        return ("d", id(self.sem)) if self.is_dma else ("e", self.eng)


class Op:
    __slots__ = ("eng", "fn", "deps", "ev", "waits", "slot")


class Sched:
    def __init__(self, nc, stack):
        self.nc = nc
        self.stack = stack
        self.ops = []
        self.per_eng = {e: [] for e in COMPUTE + ("sp",)}
        self.slots = {}
        self.eng_sem = {}
        self.tail = None
        for e in COMPUTE:
            self.eng_sem[e] = stack.enter_context(nc.semaphore("c_" + e))

    def _slot(self, name):
        if name not in self.slots:
            sem = self.stack.enter_context(self.nc.semaphore("d_" + name))
            self.slots[name] = [sem, 0, []]
        return self.slots[name]

    def peek(self, eng, reads=(), writes=(), slot=None):
        return self.add(eng, None, reads, writes, slot, _peek=True)

    def add(self, eng, fn, reads=(), writes=(), slot=None, _peek=False):
        is_dma = slot is not None

        def _flat(xs):
            out = []
            for x in xs:
                if isinstance(x, (list, tuple)):
                    out.extend(_flat(x))
                elif x not in out:
                    out.append(x)
            return out

        reads = _flat(reads)
        writes = _flat(writes)
        deps = []
        for b in reads:
            if b.w is not None:
                deps.append((b.w, "raw"))
            if b.name in PSUM_BANKS:
                for e in b.r:
                    if e.eng != eng:
                        deps.append((e, "bank"))
        for b in writes:
            if b.w is not None:
                deps.append((b.w, "waw"))
            for e in b.r:
                deps.append((e, "war"))
        op = Op()
        op.eng = eng
        op.fn = fn
        op.slot = slot
        fd = []
        seen = set()
        for (e, kind) in deps:
            if id(e) in seen:
                continue
            if (not is_dma) and (not e.is_dma) and e.eng == eng and eng == "pe":
                continue
            seen.add(id(e))
            fd.append(e)
        if _peek:
            return fd
        op.deps = fd
        ev = Ev()
        ev.is_dma = is_dma
        ev.eng = eng
        self.per_eng[eng].append(op)
        if is_dma:
            s = self._slot(slot)
            s[1] += 1
            ev.sem = s[0]
            ev.val = 16 * s[1]
            s[2].append(ev)
        else:
            ev.idx = len(self.per_eng[eng])
            ev.sem = self.eng_sem[eng]
        op.ev = ev
        self.ops.append(op)
        for b in writes:
            b.w = ev
            b.r = []
        for b in reads:
            if b not in writes:
                b.r.append(ev)
        return ev

    def seal_group(self, slot):
        s = self.slots[slot]
        for ev in s[2]:
            ev.val = 16 * s[1]

    def finalize(self):
        known = {e: {} for e in self.per_eng}
        for op in self.ops:
            k = known[op.eng]
            waits = []
            for ev in op.deps:
                key = ev.key()
                val = ev.val if ev.is_dma else ev.idx
                if k.get(key, 0) >= val:
                    continue
                waits.append(ev)
                ev.needed = True
                for kk, vv in ev.clock.items():
                    if k.get(kk, 0) < vv:
                        k[kk] = vv
                if k.get(key, 0) < val:
                    k[key] = val
            op.waits = waits
            op.ev.clock = dict(k)
        for e in COMPUTE:
            c = 0
            for op in self.per_eng[e]:
                if op.ev.is_dma:
                    continue
                if op.ev.needed:
                    c += 1
                    op.ev.val = c

    def emit(self, block):
        sched = self

        def run(engname, engobj):
            for op in sched.per_eng[engname]:
                for ev in op.waits:
                    engobj.wait_ge(ev.sem, ev.val)
                inst = op.fn(engobj)
                if op.ev.is_dma:
                    inst.then_inc(op.ev.sem, 16)
                elif op.ev.needed:
                    inst.then_inc(op.ev.sem, 1)
            if sched.tail and sched.tail[0] == engname:
                for ev in sched.tail[1]:
                    engobj.wait_ge(ev.sem, ev.val)

        @block.sync
        def _(e):
            run("sp", e)

        @block.tensor
        def _(e):
            run("pe", e)

        @block.scalar
        def _(e):
            run("act", e)

        @block.vector
        def _(e):
            run("dve", e)

        @block.gpsimd
        def _(e):
            run("pool", e)


CF = {}
_off = 0
for _n, _w in [("ident", 128), ("wqt", 512), ("dd", 512),
               ("wkv", 4), ("wkvs", 4), ("bm", 16), ("tm", 128), ("tms", 128), ("g8", 512)]:
    CF[_n] = (_off, _w)
    _off += _w
CF_W = _off


def _host_consts():
    cf = np.zeros((128, CF_W), np.float32)
    cs = np.zeros((128, 1024), np.float32)

    def put(name, arr):
        a = np.asarray(arr, np.float64)
        if name == "wqts":
            cs[:, 0:512] = a.reshape(128, 512).astype(np.float32)
            return
        if name == "dds":
            cs[:, 512:1024] = a.reshape(128, 512).astype(np.float32)
            return
        o, w = CF[name]
        cf[:, o:o + w] = a.reshape(128, w).astype(np.float32)

    i = np.arange(128)
    put("ident", np.eye(128))
    g = np.array(GAM, np.float64)
    wqt = g[:, None] ** (i[None, :] + 1.0)
    put("wqt", np.broadcast_to(wqt[None], (128, 4, 128)))
    dd = (128.0 ** -0.5) * g[None, :, None] ** (-(i[:, None, None]) - 1.0) * (i[None, None, :] >= i[:, None, None])
    put("dd", dd)
    put("wkv", (128.0 ** -0.5) * g[None, :] ** (127.0 - i[:, None]))
    t8 = i % 8
    b8 = i // 8
    wqts = g[:, None] ** (t8[None, :] + 1.0)
    put("wqts", np.broadcast_to(wqts[None], (128, 4, 128)))
    same = (b8[:, None] == b8[None, :]) & (t8[None, :] >= t8[:, None])
    dds = (128.0 ** -0.5) * g[None, :, None] ** (-(t8[:, None, None]) - 1.0) * same[:, None, :]
    put("dds", dds)
    put("wkvs", (128.0 ** -0.5) * g[None, :] ** (7.0 - t8[:, None]))
    put("bm", (b8[:, None] == np.arange(16)[None, :]).astype(np.float64))
    put("tm", (i[None, :] >= i[:, None]).astype(np.float64))
    put("tms", same.astype(np.float64))
    put("g8", np.broadcast_to((g ** 8.0)[None, :, None], (128, 4, 128)))
    inv = 10000.0 ** (-(np.arange(64, dtype=np.float64)) / 64.0)

    def rope(pos):
        ang = pos.astype(np.float64)[:, None] * inv[None, :]
        c, s = np.cos(ang), np.sin(ang)
        return np.concatenate([c, c, -s, s], axis=1).astype(np.float32)

    rope_p = rope(np.arange(SEQ)).reshape(NCH, 128, 256)
    rope_s = rope(PAST + t8)
    rope_all = np.ascontiguousarray(np.concatenate([rope_p, rope_s[None]], axis=0))
    e40 = np.zeros((40, 512), np.float32)
    for gi in range(8):
        e40[gi, gi * 64:(gi + 1) * 64] = 1.0
        e40[32 + gi, gi * 64:(gi + 1) * 64] = 1.0
    return cf, cs, rope_all, e40


PSUM_BANKS = ("PA0", "PA1", "PZ0", "PZ1", "PS", "PT", "P6", "P7")
ZORDER = [3, 4, 5, 1, 0, 2, 6]
TRACE_SCHED = None
AGE_BIAS = 0.0
PE_STALL_PEN = 200.0
TIE = 100.0
SLACK = 1.0
SYNC_LAT = 100.0
SAMPLE_POS = 4
WINDOW = 3
DMA_BW = 330.0
DMA_LAT = 2000.0


def build_program(nch=NCH, with_sample=True, window=WINDOW):
    nc = bass.Bass("TRN2", target_bir_lowering=False)

    def din(name, shape):
        return nc.dram_tensor(name, list(shape), F32, kind="ExternalInput").ap()

    def dout(name, shape):
        return nc.dram_tensor(name, list(shape), F32, kind="ExternalOutput").ap()

    xp = din("xp", [SEQ, D])
    xs = din("xs", [128, D])
    pp = din("pp", [SEQ, PLE])
    psm = din("psm", [128, PLE])
    st_in = din("st", [16, 4, 128, 128])
    w_in = din("w_in", [D, INW])
    w_out = din("w_out", [D, D])
    w_gate = din("w_gate", [D, D])
    w_ple = din("w_ple", [PLE, D])
    gT_d = din("gT", [128, 8])
    gvec_d = din("gvec", [128, 2048])
    wsT_d = din("wsT", [128, 8, 128])
    wsTs_d = din("wsTs", [128, 8, 128])
    b40_d = din("b40", [40, 256])
    cf_d = din("cf", [128, CF_W])
    cs_d = din("cs", [128, 1024])
    rope_d = din("rope", [NCH + 1, 128, 256])
    e40_d = din("e40", [40, 512])

    yp = dout("yp", [SEQ, D])
    ys = dout("ys", [128, D])
    sp_out = dout("sp_out", [4, 128, 128])
    ss_out = dout("ss_out", [16, 4, 128, 128])
    v_out = dout("v_out", [128, 512])

    st = ExitStack()
    with st:
        def sb(name, shape, dt=F32):
            return st.enter_context(nc.sbuf_tensor("s_" + name, list(shape), dt))

        def psum(name, shape, dt=F32):
            return st.enter_context(nc.psum_tensor("ps_" + name, list(shape), dt))

        S = Sched(nc, st)
        bufs = {}
        ALIAS = {"vf": ["bufA"], "b40": ["bufA"], "b40f": ["bufA"], "b40h": ["bufA"], "PA": ["PA0", "PA1"], "P67": ["P6", "P7"],
                 "rtA": ["rtA0", "rtA1", "rtA2", "rtA3"], "rtB": ["rtB0", "rtB1"],
                 "ot0": ["rtA0"], "ot1": ["rtA1"], "ot2": ["rtA2"], "ot3": ["rtA3"],
                 "ot": ["rtA0", "rtA1", "rtA2", "rtA3"],
                 "gst": ["gst0", "gst1", "gst2", "gst3"], "gmv": ["gmv0", "gmv1", "gmv2", "gmv3"],
                 "Sst": ["Sst0", "Sst1", "Sst2", "Sst3"], "VM": ["VM0", "VM1", "VM2", "VM3"]}

        def B(name):
            if name in ALIAS:
                return [B(n) for n in ALIAS[name]]
            if name not in bufs:
                bufs[name] = Buf(name)
            return bufs[name]

        win = sb("win", [128, 8, INW], BF16)
        wout = sb("wout", [128, 8, D], BF16)
        wgate = sb("wgate", [128, 8, D], BF16)
        wple = sb("wple", [128, 2, D], BF16)
        shared = sb("shared", [128, 8192], F32)
        cf = sb("cf", [128, CF_W])
        gT = sb("gT", [128, 8])
        gvec = sb("gvec", [128, 2048])
        identb = sb("identb", [128, 128], BF16)
        wsT = sb("wsTb", [128, 8, 128], BF16)
        wsTs = sb("wsTsb", [128, 8, 128], BF16)
        e40 = sb("e40b", [40, 512], BF16)
        b40l = sb("b40l", [40, 256], BF16)
        mhalf = sb("mhalf", [128, 4])

        def cfs(name):
            o, w = CF[name]
            return cf[:, o:o + w]

        xb = [sb("x%d" % i, [128, D]) for i in range(3)]
        pb = [sb("p%d" % i, [128, PLE]) for i in range(2)]
        rb = [sb("rope%d" % i, [128, 256]) for i in range(2)]
        xTb = [sb("xT%d" % i, [128, 8, 128], BF16) for i in range(2)]
        su = sb("su", [128, 512])
        sv = sb("sv", [128, 512])
        sgs = sb("sgs", [128, 512])
        rgs = sb("rgs", [128, 512])
        rtA = sb("rtA", [128, 4, 128])
        ot = rtA
        rtB = sb("rtB", [128, 4, 128])
        Qrb = [sb("Qr%d" % i, [128, 4, 128], BF16) for i in range(2)]
        Krb = [sb("Kr%d" % i, [128, 4, 128], BF16) for i in range(2)]
        Vbb = [sb("Vb%d" % i, [128, 4, 128], BF16) for i in range(2)]
        Vh = sb("Vh", [128, 4, 128], BF16)
        QKT = sb("QKT", [128, 8, 128], BF16)
        scT = sb("scT", [128, 4, 128], BF16)
        vbf = sb("vbf", [128, 512], BF16)
        tsg = sb("tsg", [128, 512])
        mix = sb("mix", [128, D], BF16)
        mixT = sb("mixT", [128, 8, 128], BF16)
        G2 = sb("G2", [128, 512])
        bufA = sb("bufA", [128, D])
        b40 = bufA[0:40, 0:256]
        b40f = bufA[0:40, 256:512]
        b40h = bufA[:].bitcast(BF16)[0:40, 1024:1280]
        vf = bufA[:, 512:1024]
        x1T = sb("x1T", [128, 8, 128], BF16)
        pTb = [sb("pT%d" % i, [128, 2, 128], BF16) for i in range(3)]
        Sst = sb("Sst", [128, 4, 128])
        Sbf = sb("Sbf", [128, 4, 128], BF16)
        ssqb = [sb("ssq_%d" % i, [128, 1]) for i in range(2)]
        msb = [sb("ms_%d" % i, [128, 1]) for i in range(2)]
        rstdb = [sb("rstd_%d" % i, [128, 1]) for i in range(2)]
        ssq2 = sb("ssq2", [128, 1]); ms2 = sb("ms2", [128, 1]); rstd2 = sb("rstd2", [128, 1])
        st6 = sb("st6", [128, 6]); mv = sb("mv", [128, 2]); ve = sb("ve", [128, 1]); rs = sb("rs", [128, 1])
        gst = sb("gst", [128, 4, 6]); gmv = sb("gmv", [128, 4, 2]); gve = sb("gve", [128, 4]); grs = sb("grs", [128, 4])

        PA = psum("PA", [128, 1024])
        PZ = [psum("PZ0", [128, 512]), psum("PZ1", [128, 512])]
        PS = psum("PS", [128, 512])
        PT = psum("PT", [128, 512])
        P67 = psum("P67", [128, 1024])
        PTb = PT[:].bitcast(BF16).rearrange("p (k n) -> p k n", k=8)
        P6 = P67[:, 0:512]
        P7 = P67[:, 512:1024]

        def OP(eng, fn, reads=(), writes=(), slot=None, cost=300.0, nbytes=0, store=False, aset=None):
            return dict(eng=eng, fn=fn, reads=list(reads), writes=list(writes), slot=slot, cost=cost,
                        nbytes=nbytes, store=store, aset=aset)

        c_mm = lambda n: 8.0 + 0.405 * n
        C_TR = 64.0
        c_act = lambda n: 200.0 + 0.87 * n
        c_dve = lambda n: 90.0 + 1.05 * n
        c_pool = lambda n: 150.0 + 2.15 * n
        C_POW = 1000.0

        setup = []
        setup.append(OP("sp", lambda e: e.dma_start(out=cf[:], in_=cf_d), writes=[B("cf")], slot="c_cf", nbytes=128 * CF_W * 4))
        setup.append(OP("sp", lambda e: e.dma_start(out=gT[:], in_=gT_d), writes=[B("gT")], slot="c_gT", nbytes=4096))
        setup.append(OP("sp", lambda e: e.dma_start(out=gvec[:], in_=gvec_d), writes=[B("gvec")], slot="c_gvec", nbytes=1 << 20))
        setup.append(OP("sp", lambda e: e.dma_start(out=b40[:], in_=b40_d), writes=[B("b40")], slot="c_b40", nbytes=40960))
        setup.append(OP("pool", lambda e: e.memset(mhalf[:], -0.5), writes=[B("mhalf")], cost=200))

        stg = [shared[:, i * 2048:(i + 1) * 2048] for i in range(4)]
        stg_cnt = [0]

        def stage_cast(src_ap, dst_ap, shape3, dst_buf):
            i = stg_cnt[0] % 4
            stg_cnt[0] += 1
            k, n = shape3
            sview = stg[i][:, 0:k * n].rearrange("p (k n) -> p k n", k=k)
            setup.append(OP("sp", lambda e: e.dma_start(out=sview, in_=src_ap), writes=[B("stg%d" % i)], slot="stg%d" % i,
                            nbytes=128 * k * n * 4))
            if stg_cnt[0] % 2 == 0:
                setup.append(OP("dve", lambda e: e.tensor_copy(out=dst_ap, in_=sview), reads=[B("stg%d" % i)], writes=[dst_buf],
                                cost=90 + 0.53 * k * n))
            else:
                setup.append(OP("act", lambda e: e.activation(out=dst_ap, in_=sview, func=AF.Copy), reads=[B("stg%d" % i)],
                                writes=[dst_buf], cost=c_act(k * n)))

        setup.append(OP("dve", lambda e: e.tensor_copy(out=identb[:], in_=cfs("ident")), reads=[B("cf")], writes=[B("identb")], cost=200))
        setup.append(OP("dve", lambda e: e.tensor_copy(out=b40h[:], in_=b40[:]), reads=[B("b40")], writes=[B("b40h")], cost=200))
        setup.append(OP("dve", lambda e: e.tensor_copy(out=b40f[:], in_=b40h[:]), reads=[B("b40h")], writes=[B("b40f")], cost=200))
        setup.append(OP("dve", lambda e: e.tensor_copy(out=b40l[0:32, :], in_=b40h[0:32, :]), reads=[B("b40h")], writes=[B("b40l")], cost=200))
        setup.append(OP("dve", lambda e: e.tensor_tensor(out=b40l[32:40, :], in0=b40[32:40, :], in1=b40f[32:40, :], op=ALU.subtract),
                        reads=[B("b40"), B("b40f")], writes=[B("b40l")], cost=200))

        w_in_v = w_in.rearrange("(k p) n -> p k n", p=128)

        def load_ws(src, mask_name, dst, dname):
            i = stg_cnt[0] % 4
            stg_cnt[0] += 1
            sview = stg[i][:, 0:1024].rearrange("p (g n) -> p g n", g=8)
            setup.append(OP("sp", lambda e: e.dma_start(out=sview, in_=src), writes=[B("stg%d" % i)], slot="stg%d" % i, nbytes=1 << 19))
            setup.append(OP("dve", lambda e: e.tensor_tensor(out=dst[:], in0=sview,
                                                             in1=cfs(mask_name).unsqueeze(1).to_broadcast([128, 8, 128]), op=ALU.mult),
                            reads=[B("stg%d" % i), B("cf")], writes=[B(dname)], cost=c_dve(1024)))

        first = True
        for nb in ZORDER:
            for kh in range(2):
                stage_cast(w_in_v[:, 4 * kh:4 * kh + 4, nb * 512:(nb + 1) * 512],
                           win[:, 4 * kh:4 * kh + 4, nb * 512:(nb + 1) * 512], (4, 512), B("win%d" % nb))
            if first:
                first = False
                load_ws(wsT_d, "tm", wsT, "wsT")
                i0 = stg_cnt[0] % 4
                stg_cnt[0] += 1
                e40s = stg[i0][0:40, 0:512]
                setup.append(OP("sp", lambda e: e.dma_start(out=e40s, in_=e40_d), writes=[B("stg%d" % i0)], slot="stg%d" % i0, nbytes=81920))
                setup.append(OP("dve", lambda e: e.tensor_copy(out=e40[:], in_=e40s), reads=[B("stg%d" % i0)], writes=[B("e40")], cost=300))
        w_out_v = w_out.rearrange("(k p) n -> p k n", p=128)
        for c in range(4):
            stage_cast(w_out_v[:, 2 * c:2 * c + 2, :], wout[:, 2 * c:2 * c + 2, :], (2, 1024), B("wout"))
        w_gate_v = w_gate.rearrange("(k p) n -> p k n", p=128)
        for c in range(4):
            stage_cast(w_gate_v[:, 2 * c:2 * c + 2, :], wgate[:, 2 * c:2 * c + 2, :], (2, 1024), B("wgate"))
        w_ple_v = w_ple.rearrange("(k p) n -> p k n", p=128)
        stage_cast(w_ple_v, wple[:], (2, 1024), B("wple"))
        load_ws(wsTs_d, "tms", wsTs, "wsTs")

        shb = shared[:].bitcast(BF16)
        S0f = [shared[:, i * 512:(i + 1) * 512] for i in range(8)]
        S0bf = [shb[:, 8192 + j * 512:8192 + (j + 1) * 512] for j in range(3)]
        oTs = shared[:, 4864:5376]
        VM = shb[:, 10752:12800].rearrange("p (b n) -> p b n", b=4)
        CSB = shared[:, 7168:8192]
        SHARED_ALL = [B("stg%d" % i) for i in range(4)]

        def tile_prog(pos, t):
            ops = []
            A = lambda *a, **k: ops.append(OP(*a, **k))
            sample = (t == NCH)
            xi = pos % 3
            r2 = pos % 2
            X = xb[xi]; BX = B("x%d" % xi)
            Pt = pb[r2]; BP = B("p%d" % r2)
            RP = rb[r2]; BR = B("rope%d" % r2)
            pT = pTb[xi]; BPT = B("pT%d" % xi)
            xT = xTb[r2]; BXT = B("xT%d" % r2)
            Qr = Qrb[r2]; Kr = Krb[r2]; Vb = Vbb[r2]
            ssq = ssqb[r2]; ms = msb[r2]; rstd = rstdb[r2]
            NSSQ, NMS, NRSTD = "ssq_%d" % r2, "ms_%d" % r2, "rstd_%d" % r2
            NQ, NK, NV = "Qr%d" % r2, "Kr%d" % r2, "Vb%d" % r2
            cos2 = RP[:, 0:128]
            sin2 = RP[:, 128:256]
            if not sample:
                A("sp", lambda e: e.dma_start(out=X[:], in_=xp[t * 128:(t + 1) * 128, :]), writes=[BX], slot="x%d" % xi, nbytes=1 << 19, cost=100)
                A("sp", lambda e: e.dma_start(out=Pt[:], in_=pp[t * 128:(t + 1) * 128, :]), writes=[BP], slot="p%d" % r2, nbytes=1 << 17, cost=100)
            else:
                A("sp", lambda e: e.dma_start(out=X[:], in_=xs), writes=[BX], slot="x%d" % xi, nbytes=1 << 19, cost=100)
                A("sp", lambda e: e.dma_start(out=Pt[:], in_=psm), writes=[BP], slot="p%d" % r2, nbytes=1 << 17, cost=100)
            A("sp", lambda e: e.dma_start(out=RP[:], in_=rope_d[t]), writes=[BR], slot="rope%d" % r2, nbytes=1 << 17, cost=100)
            if sample:
                CS = CSB; BCS = B("csb")
                A("sp", lambda e: e.dma_start(out=CS, in_=cs_d), writes=[BCS] + SHARED_ALL, slot="c_cs", nbytes=1 << 19, cost=100)
                WQT = CS[:, 0:512].rearrange("p (h n) -> p h n", h=4)
                DD = CS[:, 512:1024].rearrange("p (h n) -> p h n", h=4)
                BWQ = BCS
                WKV = cfs("wkvs")
                WS = wsTs; BWS = B("wsTs"); bcol = 128
            else:
                WQT = cfs("wqt").rearrange("p (h n) -> p h n", h=4)
                DD = cfs("dd").rearrange("p (h n) -> p h n", h=4)
                BWQ = B("cf")
                WKV = cfs("wkv")
                WS = wsT; BWS = B("wsT"); bcol = 0

            A("act", lambda e: e.activation(out=xT[:].rearrange("p k n -> p (k n)"), in_=X[:], func=AF.Square, accum_out=ssq[:]),
              reads=[BX], writes=[BXT, B(NSSQ)], cost=c_act(1024) + 100)
            A("pool", lambda e: e.tensor_scalar(out=ms[:], in0=ssq[:], scalar1=1.0 / D, scalar2=RMS_EPS, op0=ALU.mult, op1=ALU.add),
              reads=[B(NSSQ)], writes=[B(NMS)], cost=200)
            A("pool", lambda e: e.tensor_tensor(out=rstd[:], in0=ms[:], in1=mhalf[:, 0:1], op=ALU.pow),
              reads=[B(NMS), B("mhalf")], writes=[B(NRSTD)], cost=550)
            PAv = PA[:].rearrange("p (k n) -> p k n", k=8)
            for k in range(8):
                A("pe", lambda e, k=k: e.transpose(out=PAv[:, k, :], in_=X[:, k * 128:(k + 1) * 128], identity=cfs("ident")),
                  reads=[BX, B("cf")], writes=[B("PA")], cost=C_TR)
            A("dve", lambda e: e.tensor_tensor(out=xT[:], in0=PAv, in1=gT[:].unsqueeze(2).to_broadcast([128, 8, 128]), op=ALU.mult),
              reads=[B("PA"), B("gT")], writes=[BXT], cost=c_dve(1024))
            PTf = PT[:, 0:256].rearrange("p (k n) -> p k n", k=2)
            for k in range(2):
                A("pe", lambda e, k=k: e.transpose(out=PTf[:, k, :], in_=Pt[:, k * 128:(k + 1) * 128], identity=cfs("ident")),
                  reads=[BP, B("cf")], writes=[B("PT")], cost=C_TR)
            A("dve", lambda e: e.tensor_copy(out=pT[:], in_=PTf), reads=[B("PT")], writes=[BPT], cost=c_dve(256))

            def zblock(zi, nb):
                zb = PZ[zi % 2][:]
                BZ = B("PZ%d" % (zi % 2))
                for k in range(8):
                    A("pe", lambda e, k=k: e.matmul(out=zb, lhsT=xT[:, k, :], rhs=win[:, k, nb * 512:(nb + 1) * 512],
                                                    start=(k == 0), stop=(k == 7)),
                      reads=[BXT, B("win%d" % nb)], writes=[BZ], cost=c_mm(512))
                return zb, BZ

            for zi, nb in enumerate(ZORDER):
                zb, BZ = zblock(zi, nb)
                zb3 = zb.rearrange("p (h n) -> p h n", h=4)
                if nb == 1:
                    A("act", lambda e, zb=zb: e.activation(out=sv[:], in_=zb, func=AF.Gelu_apprx_tanh, scale=rstd[:]),
                      reads=[BZ, B(NRSTD)], writes=[B("sv")], cost=c_act(512) + 100, aset="gelu")
                    A("dve", lambda e: e.bn_stats(out=st6[:], in_=sv[:]), reads=[B("sv")], writes=[B("st6")], cost=c_dve(512))
                    A("dve", lambda e: e.bn_aggr(out=mv[:], in_=st6[:]), reads=[B("st6")], writes=[B("mv")], cost=200)
                    A("pool", lambda e: e.tensor_scalar(out=ve[:], in0=mv[:, 1:2], scalar1=LN_EPS, scalar2=None, op0=ALU.add),
                      reads=[B("mv")], writes=[B("ve")], cost=200)
                    A("pool", lambda e: e.tensor_tensor(out=rs[:], in0=ve[:], in1=mhalf[:, 0:1], op=ALU.pow),
                      reads=[B("ve"), B("mhalf")], writes=[B("rs")], cost=550)
                    A("dve", lambda e: e.tensor_scalar(out=sv[:], in0=sv[:], scalar1=mv[:, 0:1], scalar2=rs[:], op0=ALU.subtract, op1=ALU.mult),
                      reads=[B("sv"), B("mv"), B("rs")], writes=[B("sv")], cost=c_dve(512))
                    if not sample:
                        A("dve", lambda e: e.tensor_tensor(out=vbf[:], in0=sv[:], in1=gvec[:, 1024:1536], op=ALU.mult),
                          reads=[B("sv"), B("gvec")], writes=[B("vbf")], cost=c_dve(512))
                    else:
                        A("pool", lambda e: e.tensor_tensor(out=vf, in0=sv[:], in1=gvec[:, 1024:1536], op=ALU.mult),
                          reads=[B("sv"), B("gvec")], writes=[B("vf")], cost=c_pool(512))
                        A("sp", lambda e: e.dma_start(out=v_out, in_=vf), reads=[B("vf")], writes=[B("v_out")], slot="o_v",
                          nbytes=1 << 18, cost=100, store=True)
                        A("pool", lambda e: e.tensor_copy(out=vbf[:], in_=vf), reads=[B("vf")], writes=[B("vbf")], cost=1900)
                elif nb == 0:
                    A("act", lambda e, zb=zb: e.activation(out=su[:], in_=zb, func=AF.Gelu_apprx_tanh, scale=rstd[:]),
                      reads=[BZ, B(NRSTD)], writes=[B("su")], cost=c_act(512) + 100, aset="gelu")
                elif nb == 2:
                    A("act", lambda e, zb=zb: e.activation(out=sgs[:], in_=zb, func=AF.Silu, scale=rstd[:]),
                      reads=[BZ, B(NRSTD)], writes=[B("sgs")], cost=c_act(512) + 100, aset="silu")
                    A("pool", lambda e: e.tensor_tensor(out=tsg[:], in0=su[:], in1=sgs[:], op=ALU.mult),
                      reads=[B("su"), B("sgs")], writes=[B("tsg")], cost=c_pool(512))
                elif nb in (3, 4):
                    dst = Qr if nb == 3 else Kr
                    BD = B(NQ) if nb == 3 else B(NK)
                    A("dve", lambda e, zb3=zb3: e.scalar_tensor_tensor(out=rtA[:], in0=zb3, scalar=rstd[:],
                                                                       in1=cos2.unsqueeze(1).to_broadcast([128, 4, 128]),
                                                                       op0=ALU.mult, op1=ALU.mult),
                      reads=[BZ, B(NRSTD), BR], writes=[B("rtA")], cost=c_dve(512))
                    A("dve", lambda e, zb3=zb3: e.scalar_tensor_tensor(out=rtB[:, :, 0:64], in0=zb3[:, :, 64:128], scalar=rstd[:],
                                                                       in1=sin2[:, 0:64].unsqueeze(1).to_broadcast([128, 4, 64]),
                                                                       op0=ALU.mult, op1=ALU.mult),
                      reads=[BZ, B(NRSTD), BR], writes=[B("rtB0")], cost=c_dve(256))
                    A("dve", lambda e, zb3=zb3: e.scalar_tensor_tensor(out=rtB[:, :, 64:128], in0=zb3[:, :, 0:64], scalar=rstd[:],
                                                                       in1=sin2[:, 64:128].unsqueeze(1).to_broadcast([128, 4, 64]),
                                                                       op0=ALU.mult, op1=ALU.mult),
                      reads=[BZ, B(NRSTD), BR], writes=[B("rtB1")], cost=c_dve(256))
                    A("pool", lambda e, dst=dst: e.tensor_tensor(out=dst[:], in0=rtA[:], in1=rtB[:], op=ALU.add),
                      reads=[B("rtA"), B("rtB")], writes=[BD], cost=c_pool(512))
                elif nb == 5:
                    A("act", lambda e, zb3=zb3: e.activation(out=Vb[:], in_=zb3, func=AF.Identity, scale=rstd[:]),
                      reads=[BZ, B(NRSTD)], writes=[B(NV)], cost=c_act(512) + 100)
                    A("pool", lambda e: e.tensor_tensor(out=Vh[:], in0=Vb[:], in1=WKV.unsqueeze(2).to_broadcast([128, 4, 128]), op=ALU.mult),
                      reads=[B(NV), B("cf")], writes=[B("Vh")], cost=1050)
                elif nb == 6:
                    A("act", lambda e, zb=zb: e.activation(out=rgs[:], in_=zb, func=AF.Silu, scale=rstd[:]),
                      reads=[BZ, B(NRSTD)], writes=[B("rgs")], cost=c_act(512) + 100, aset="silu")
                    A("pool", lambda e: e.tensor_tensor(out=G2[:], in0=rgs[:], in1=gvec[:, 1536:2048], op=ALU.mult),
                      reads=[B("rgs"), B("gvec")], writes=[B("G2")], cost=c_pool(512))
                if nb == 2:
                    A("pe", lambda e: e.matmul(out=PS[:], lhsT=b40l[:, bcol:bcol + 128], rhs=e40[:], start=True, stop=False),
                      reads=[B("b40l"), B("e40")], writes=[B("PS")], cost=c_mm(512))
                    for g in range(8):
                        A("pe", lambda e, g=g: e.matmul(out=PS[:, g * 64:(g + 1) * 64], lhsT=WS[:, g, :], rhs=vbf[:, g * 64:(g + 1) * 64],
                                                        start=False, stop=(g == 7)),
                          reads=[BWS, B("vbf")], writes=[B("PS")], cost=c_mm(64))
                    A("dve", lambda e: e.tensor_tensor(out=mix[:, 0:512], in0=tsg[:], in1=PS[:], op=ALU.mult),
                      reads=[B("tsg"), B("PS")], writes=[B("mixA")], cost=c_dve(512))

            for h in range(4):
                A("pe", lambda e, h=h: e.transpose(out=PTb[:, h, :], in_=Qr[:, h, :], identity=identb[:]),
                  reads=[B(NQ), B("identb")], writes=[B("PT")], cost=C_TR)
            for h in range(4):
                A("pe", lambda e, h=h: e.transpose(out=PTb[:, 4 + h, :], in_=Kr[:, h, :], identity=identb[:]),
                  reads=[B(NK), B("identb")], writes=[B("PT")], cost=C_TR)
            A("dve", lambda e: e.tensor_tensor(out=QKT[:, 0:4, :], in0=PTb[:, 0:4, :], in1=WQT, op=ALU.mult),
              reads=[B("PT"), BWQ], writes=[B("QT")], cost=c_dve(512))
            A("act", lambda e: e.activation(out=QKT[:, 4:8, :], in_=PTb[:, 4:8, :], func=AF.Copy),
              reads=[B("PT")], writes=[B("KT")], cost=c_act(512))
            PSv = PS[:].rearrange("p (h n) -> p h n", h=4)
            P6v = P6.rearrange("p (h n) -> p h n", h=4)
            P7v = P7.rearrange("p (h n) -> p h n", h=4)
            for h in range(4):
                A("pe", lambda e, h=h: e.matmul(out=PSv[:, h, :], lhsT=QKT[:, 4 + h, :], rhs=QKT[:, h, :], start=True, stop=True),
                  reads=[B("QT"), B("KT")], writes=[B("PS")], cost=c_mm(128))
            A("dve", lambda e: e.tensor_tensor(out=scT[:], in0=PSv, in1=DD, op=ALU.mult),
              reads=[B("PS"), BWQ], writes=[B("scT")], cost=c_dve(512))

            if not sample:
                for h in range(4):
                    A("pe", lambda e, h=h: e.matmul(out=P6v[:, h, :], lhsT=Kr[:, h, :], rhs=Vh[:, h, :], start=True, stop=True),
                      reads=[B(NK), B("Vh")], writes=[B("P6")], cost=c_mm(128))
                for h in range(4):
                    A("pe", lambda e, h=h: e.matmul(out=P7v[:, h, :], lhsT=scT[:, h, :], rhs=Vb[:, h, :], start=True, stop=(t == 0)),
                      reads=[B("scT"), B(NV)], writes=[B("P7")], cost=c_mm(128))
                    if t > 0:
                        A("pe", lambda e, h=h: e.matmul(out=P7v[:, h, :], lhsT=QKT[:, h, :], rhs=Sbf[:, h, :], start=False, stop=True),
                          reads=[B("QT"), B("Sbf")], writes=[B("P7")], cost=c_mm(128))
                if t == 0:
                    A("dve", lambda e: e.tensor_copy(out=Sst[:], in_=P6v), reads=[B("P6")], writes=[B("Sst")], cost=c_dve(512))
                else:
                    for h in range(4):
                        A("dve", lambda e, h=h: e.scalar_tensor_tensor(out=Sst[:, h, :], in0=Sst[:, h, :], scalar=float(GAM[h] ** 128),
                                                                       in1=P6v[:, h, :], op0=ALU.mult, op1=ALU.add),
                          reads=[B("P6"), B("Sst%d" % h)], writes=[B("Sst%d" % h)], cost=300)
                if t < nch - 1:
                    A("act", lambda e: e.activation(out=Sbf[:], in_=Sst[:], func=AF.Copy), reads=[B("Sst")], writes=[B("Sbf")], cost=c_act(512))
                else:
                    A("sp", lambda e: e.dma_start(out=sp_out.rearrange("h d e -> d h e"), in_=Sst[:]),
                      reads=[B("Sst")], writes=[B("sp_out")], slot="o_sp", nbytes=1 << 18, cost=100, store=True)
            else:
                G8 = cfs("g8")
                BMc = cfs("bm")
                for b in range(16):
                    ks = b % 8
                    js = b % 3
                    S0 = S0f[ks]
                    BS0 = B("S0f%d" % ks)
                    S0b = S0bf[js]
                    BS0b = B("S0bf%d" % js)
                    S0v = S0.rearrange("p (h n) -> p h n", h=4)
                    A("sp", lambda e, b=b, S0v=S0v: e.dma_start(out=S0v, in_=st_in[b].rearrange("h d e -> d h e")),
                      writes=[BS0] + (SHARED_ALL if b < 8 else []), slot="s0_%d" % ks, nbytes=1 << 18, cost=100)
                    A("act", lambda e, S0=S0, S0b=S0b: e.activation(out=S0b, in_=S0, func=AF.Copy),
                      reads=[BS0], writes=[BS0b] + (SHARED_ALL if b < 3 else []), cost=c_act(512))
                    if b % 4 == 0:
                        for bb in range(4):
                            A("dve", lambda e, b=b, bb=bb: e.tensor_scalar(out=VM[:, bb, :], in0=Vh[:].rearrange("p h n -> p (h n)"),
                                                                           scalar1=BMc[:, b + bb:b + bb + 1], scalar2=None, op0=ALU.mult),
                              reads=[B("Vh"), B("cf")], writes=[B("VM%d" % bb)] + (SHARED_ALL if b == 0 else []), cost=300)
                    PU = PZ[b % 2][:]
                    BPU = B("PZ%d" % (b % 2))
                    PUv = PU.rearrange("p (h n) -> p h n", h=4)
                    VMv = VM[:, b % 4, :].rearrange("p (h n) -> p h n", h=4)
                    for h in range(4):
                        A("pe", lambda e, h=h, PUv=PUv, VMv=VMv: e.matmul(out=PUv[:, h, :], lhsT=Kr[:, h, :], rhs=VMv[:, h, :], start=True, stop=True),
                          reads=[B(NK), B("VM%d" % (b % 4))], writes=[BPU], cost=c_mm(128))
                    for h in range(4):
                        A("pe", lambda e, h=h, b=b, S0b=S0b: e.matmul(out=P6[:, h * 128 + 8 * b:h * 128 + 8 * b + 8],
                                                                      lhsT=S0b[:, h * 128:(h + 1) * 128], rhs=QKT[:, h, 8 * b:8 * b + 8],
                                                                      start=True, stop=True),
                          reads=[BS0b, B("QT")], writes=[B("P6")], cost=70)
                    A("pool", lambda e, S0=S0: e.tensor_tensor(out=S0, in0=S0, in1=G8, op=ALU.mult),
                      reads=[BS0, B("cf")], writes=[BS0], cost=c_pool(512))
                    A("dve", lambda e, S0=S0, PU=PU: e.tensor_tensor(out=S0, in0=S0, in1=PU, op=ALU.add),
                      reads=[BS0, BPU], writes=[BS0], cost=c_dve(512))
                    A("sp", lambda e, b=b, S0v=S0v: e.dma_start(out=ss_out[b].rearrange("h d e -> d h e"), in_=S0v),
                      reads=[BS0], writes=[B("ss_out")], slot="o_ss%d" % ks, nbytes=1 << 18, cost=100, store=True)
                A("act", lambda e: e.activation(out=oTs, in_=P6, func=AF.Copy), reads=[B("P6")], writes=[B("oTs")] + SHARED_ALL, cost=c_act(512))
                for h in range(4):
                    A("pe", lambda e, h=h: e.matmul(out=P7v[:, h, :], lhsT=scT[:, h, :], rhs=Vb[:, h, :], start=True, stop=False),
                      reads=[B("scT"), B(NV)], writes=[B("P7")], cost=c_mm(128))
                    A("pe", lambda e, h=h: e.matmul(out=P7v[:, h, :], lhsT=oTs[:, h * 128:(h + 1) * 128], rhs=cfs("ident"), start=False, stop=True),
                      reads=[B("oTs"), B("cf")], writes=[B("P7")], cost=4 * c_mm(128))

            for h in range(4):
                A("dve", lambda e, h=h: e.bn_stats(out=gst[:, h, :], in_=P7v[:, h, :]), reads=[B("P7")], writes=[B("gst%d" % h)], cost=210)
            for h in range(4):
                A("dve", lambda e, h=h: e.bn_aggr(out=gmv[:, h, :], in_=gst[:, h, :]), reads=[B("gst%d" % h)], writes=[B("gmv%d" % h)], cost=100)
            A("pool", lambda e: e.tensor_scalar(out=gve[:], in0=gmv[:, :, 1], scalar1=LN_EPS, scalar2=None, op0=ALU.add),
              reads=[B("gmv")], writes=[B("gve")], cost=200)
            A("pool", lambda e: e.tensor_tensor(out=grs[:], in0=gve[:], in1=mhalf[:], op=ALU.pow),
              reads=[B("gve"), B("mhalf")], writes=[B("grs")], cost=C_POW)
            for h in range(4):
                A("dve", lambda e, h=h: e.tensor_scalar(out=ot[:, h, :], in0=P7v[:, h, :], scalar1=gmv[:, h, 0:1], scalar2=grs[:, h:h + 1],
                                                        op0=ALU.subtract, op1=ALU.mult),
                  reads=[B("P7"), B("gmv%d" % h), B("grs")], writes=[B("ot%d" % h)], cost=340)
            A("pool", lambda e: e.tensor_tensor(out=mix[:, 512:1024], in0=ot[:].rearrange("p h n -> p (h n)"), in1=G2[:], op=ALU.mult),
              reads=[B("ot"), B("G2")], writes=[B("mixB")], cost=c_pool(512))

            for k in range(8):
                A("pe", lambda e, k=k: e.transpose(out=PTb[:, k, :], in_=mix[:, k * 128:(k + 1) * 128], identity=identb[:]),
                  reads=[B("mixA"), B("mixB"), B("identb")], writes=[B("PT")], cost=C_TR)
            A("act", lambda e: e.activation(out=mixT[:], in_=PTb, func=AF.Copy), reads=[B("PT")], writes=[B("mixT")], cost=c_act(1024))
            for n in range(2):
                for k in range(8):
                    A("pe", lambda e, k=k, n=n: e.matmul(out=P67[:, n * 512:(n + 1) * 512], lhsT=mixT[:, k, :], rhs=wout[:, k, n * 512:(n + 1) * 512],
                                                         start=(k == 0), stop=(k == 7)),
                      reads=[B("mixT"), B("wout")], writes=[B("P6") if n == 0 else B("P7")], cost=c_mm(512))
            A("act", lambda e: e.activation(out=bufA[:], in_=P67[:], func=AF.Square, accum_out=ssq2[:]),
              reads=[B("P67")], writes=[B("bufA"), B("ssq2")], cost=c_act(1024) + 100)
            A("pool", lambda e: e.tensor_scalar(out=ms2[:], in0=ssq2[:], scalar1=1.0 / D, scalar2=RMS_EPS, op0=ALU.mult, op1=ALU.add),
              reads=[B("ssq2")], writes=[B("ms2")], cost=200)
            A("pool", lambda e: e.tensor_tensor(out=rstd2[:], in0=ms2[:], in1=mhalf[:, 0:1], op=ALU.pow),
              reads=[B("ms2"), B("mhalf")], writes=[B("rstd2")], cost=550)
            A("dve", lambda e: e.scalar_tensor_tensor(out=bufA[:], in0=P67[:], scalar=rstd2[:], in1=gvec[:, 0:1024], op0=ALU.mult, op1=ALU.mult),
              reads=[B("P67"), B("rstd2"), B("gvec")], writes=[B("bufA")], cost=c_dve(1024))
            A("dve", lambda e: e.tensor_tensor(out=X[:], in0=X[:], in1=bufA[:], op=ALU.add),
              reads=[BX, B("bufA")], writes=[BX], cost=c_dve(1024))
            for k in range(8):
                A("pe", lambda e, k=k: e.transpose(out=PAv[:, k, :], in_=X[:, k * 128:(k + 1) * 128], identity=cfs("ident")),
                  reads=[BX, B("cf")], writes=[B("PA")], cost=C_TR)
            A("act", lambda e: e.activation(out=x1T[:], in_=PAv, func=AF.Copy), reads=[B("PA")], writes=[B("x1T")], cost=c_act(1024))
            for n in range(2):
                for k in range(8):
                    A("pe", lambda e, k=k, n=n: e.matmul(out=P67[:, n * 512:(n + 1) * 512], lhsT=x1T[:, k, :], rhs=wgate[:, k, n * 512:(n + 1) * 512],
                                                         start=(k == 0), stop=(k == 7)),
                      reads=[B("x1T"), B("wgate")], writes=[B("P6") if n == 0 else B("P7")], cost=c_mm(512))
            for n in range(2):
                for k in range(2):
                    A("pe", lambda e, k=k, n=n: e.matmul(out=PA[:, n * 512:(n + 1) * 512], lhsT=pT[:, k, :], rhs=wple[:, k, n * 512:(n + 1) * 512],
                                                         start=(k == 0), stop=(k == 1)),
                      reads=[BPT, B("wple")], writes=[B("PA")], cost=c_mm(512))
            A("act", lambda e: e.activation(out=bufA[:], in_=P67[:], func=AF.Tanh, scale=0.5), reads=[B("P67")], writes=[B("bufA")], cost=c_act(1024))
            A("dve", lambda e: e.scalar_tensor_tensor(out=bufA[:], in0=bufA[:], scalar=1.0, in1=PA[:], op0=ALU.add, op1=ALU.mult),
              reads=[B("bufA"), B("PA")], writes=[B("bufA")], cost=c_dve(1024))
            A("dve", lambda e: e.scalar_tensor_tensor(out=X[:], in0=bufA[:], scalar=0.5, in1=X[:], op0=ALU.mult, op1=ALU.add),
              reads=[B("bufA"), BX], writes=[BX], cost=c_dve(1024))
            dsty = ys if sample else yp[t * 128:(t + 1) * 128, :]
            A("sp", lambda e: e.dma_start(out=dsty, in_=X[:]), reads=[BX], writes=[B("y_out%d" % xi)], slot="oy%d" % xi,
              nbytes=1 << 19, cost=100, store=True)
            return ops

        tiles = list(range(nch))
        if with_sample:
            tiles.insert(min(SAMPLE_POS, nch), NCH)
        progs = [setup] + [tile_prog(i, t) for i, t in enumerate(tiles)]

        def _flatb(xs):
            out = []
            for x in xs:
                if isinstance(x, (list, tuple)):
                    out.extend(_flatb(x))
                else:
                    out.append(x)
            return out

        for prog in progs:
            for d in prog:
                d["reads"] = _flatb(d["reads"])
                d["writes"] = _flatb(d["writes"])
                d["rset"] = set(id(b) for b in d["reads"])
                d["wset"] = set(id(b) for b in d["writes"])

        npred = []
        succs = []
        for prog in progs:
            lastw = {}
            readers = {}
            np_ = [0] * len(prog)
            sc = [[] for _ in prog]
            for i, d in enumerate(prog):
                ps = set()
                for bid in d["rset"]:
                    if bid in lastw:
                        ps.add(lastw[bid])
                for bid in d["wset"]:
                    if bid in lastw:
                        ps.add(lastw[bid])
                    for j in readers.get(bid, ()):
                        ps.add(j)
                ps.discard(i)
                if prog is setup and i > 0:
                    ps.add(i - 1)
                for j in ps:
                    sc[j].append(i)
                np_[i] = len(ps)
                for bid in d["wset"]:
                    lastw[bid] = i
                    readers[bid] = []
                for bid in d["rset"]:
                    if bid not in d["wset"]:
                        readers.setdefault(bid, []).append(i)
            npred.append(np_)
            succs.append(sc)
        ready = [sorted(i for i in range(len(p)) if npred[k][i] == 0) for k, p in enumerate(progs)]
        left = [len(p) for p in progs]

        pending = {}
        for d in setup:
            for bid in d["wset"]:
                pending[bid] = pending.get(bid, 0) + 1

        FLOW = set(id(b) for b in _flatb([B("Sst"), B("Sbf")]))
        tile_written = set()
        for pi in range(1, len(progs)):
            for d in progs[pi]:
                tile_written |= d["wset"]
        seglen = {}
        segops = {}
        flow_left = {}
        for pi in range(1, len(progs)):
            acc = {}
            for oi, d in enumerate(progs[pi]):
                for bid in d["rset"] | d["wset"]:
                    if bid in tile_written:
                        acc.setdefault(bid, []).append((oi, bid in d["rset"], bid in d["wset"]))
            for bid, lst in acc.items():
                if bid in FLOW:
                    flow_left.setdefault(bid, {})[pi] = len(lst)
                    continue
                st0 = 0
                had_read = False
                for k, (oi, isr, isw) in enumerate(lst):
                    fresh = isw and not isr
                    if k > 0 and fresh and had_read:
                        for kk in range(st0, k):
                            seglen[(pi, lst[kk][0], bid)] = k - st0
                            segops[(pi, lst[kk][0], bid)] = [x[0] for x in lst[st0:k]]
                        st0 = k
                        had_read = False
                    if isr:
                        had_read = True
                for kk in range(st0, len(lst)):
                    seglen[(pi, lst[kk][0], bid)] = len(lst) - st0
                    segops[(pi, lst[kk][0], bid)] = [x[0] for x in lst[st0:]]
        holder = {}

        bank_ids = set(id(bufs[n]) for n in PSUM_BANKS if n in bufs)

        def eligible(pi, oi, d):
            for bid in d["rset"] | d["wset"]:
                if pending.get(bid, 0) > 0:
                    return False
                if bid not in tile_written:
                    continue
                if bid in FLOW:
                    for pj, lf in flow_left[bid].items():
                        if pj < pi and lf > 0:
                            return False
                else:
                    h = holder.get(bid)
                    if h is not None and h[0] != pi:
                        return False
                    if h is None and bid in bank_ids:
                        for oj in segops[(pi, oi, bid)]:
                            dj = progs[pi][oj]
                            for b2 in dj["rset"] | dj["wset"]:
                                h2 = holder.get(b2)
                                if h2 is not None and h2[0] != pi:
                                    return False
            return True

        eng_free = {e: 0.0 for e in COMPUTE + ("sp",)}
        dma_free = [0.0]
        act_set = [None]
        store_events = []
        active = [0] + list(range(1, min(len(progs), 1 + window)))
        nxt = 1 + window
        while active:
            best = None
            best_key = None
            for pi in active:
                for oi in ready[pi]:
                    d = progs[pi][oi]
                    if pi != 0 and not eligible(pi, oi, d):
                        continue
                    deps = S.peek(d["eng"], d["reads"], d["writes"], d["slot"])
                    t_ready = max([ev.t_done for ev in deps], default=0.0)
                    start = max(t_ready, eng_free[d["eng"]])
                    key = (start + AGE_BIAS * (pi - active[1 if (len(active) > 1 and active[0] == 0) else 0]), pi, oi)
                    if best is None or key[0] < best_key[0] - TIE or (abs(key[0] - best_key[0]) <= TIE and key[1:] < best_key[1:]):
                        best, best_key, best_start = (pi, oi), key, start
                    if pi == 0:
                        break
            if best is None:
                names = {id(b): n for n, b in bufs.items()}
                msg = []
                for pi in active:
                    for oi in ready[pi][:6]:
                        d = progs[pi][oi]
                        why = []
                        for bid in d["rset"] | d["wset"]:
                            if pending.get(bid, 0) > 0:
                                why.append("pending:" + names.get(bid, "?"))
                            h = holder.get(bid)
                            if h is not None and h[0] != pi:
                                why.append("held:%s by %d (%d left)" % (names.get(bid, "?"), h[0], h[1]))
                            if bid in FLOW:
                                why.append("flow:" + names.get(bid, "?") + str(flow_left[bid]))
                        msg.append("tile %d op %d %s: %s" % (pi, oi, d["eng"], why))
                raise RuntimeError("list scheduler deadlock\n" + "\n".join(msg))
            pi, oi = best
            d = progs[pi][oi]
            if pi == 0:
                for bid in d["wset"]:
                    pending[bid] -= 1
            else:
                for bid in d["rset"] | d["wset"]:
                    if bid not in tile_written:
                        continue
                    if bid in FLOW:
                        flow_left[bid][pi] -= 1
                    else:
                        h = holder.get(bid)
                        if h is None:
                            h = holder[bid] = [pi, seglen[(pi, oi, bid)]]
                        h[1] -= 1
                        if h[1] == 0:
                            del holder[bid]
            cost = d["cost"]
            if d["eng"] == "pe" and best_start > eng_free["pe"] + 40.0:
                cost = cost + PE_STALL_PEN
            if d.get("aset") is not None and d["eng"] == "act":
                if act_set[0] is not None and d["aset"] != act_set[0]:
                    cost = cost + 1300.0
                act_set[0] = d["aset"]
            if TRACE_SCHED is not None:
                TRACE_SCHED.append((pi, oi, d["eng"], best_start, cost, eng_free[d["eng"]], None, None))
            ev = S.add(d["eng"], d["fn"], d["reads"], d["writes"], d["slot"])
            if d["slot"] is not None:
                eng_free[d["eng"]] = best_start + cost
                xfer = d["nbytes"] / DMA_BW
                t0 = max(best_start, dma_free[0])
                dma_free[0] = t0 + xfer
                ev.t_done = t0 + xfer + DMA_LAT
            else:
                eng_free[d["eng"]] = best_start + cost
                ev.t_done = best_start + cost * SLACK + SYNC_LAT
            if d["store"]:
                store_events.append(ev)
            ready[pi].remove(oi)
            left[pi] -= 1
            for j in succs[pi][oi]:
                npred[pi][j] -= 1
                if npred[pi][j] == 0:
                    ready[pi].append(j)
            ready[pi].sort()
            if left[pi] == 0:
                active.remove(pi)
                if nxt < len(progs):
                    active.append(nxt)
                    nxt += 1
        build_program.est_ns = max(eng_free.values())

        S.finalize()
        S.tail = ("sp", store_events)
        with nc.Block() as block:
            S.emit(block)
    return nc


_PROG = {}


def _get_program():
    if "nc" not in _PROG:
        _PROG["nc"] = build_program()
    return _PROG["nc"]


def _prep(x_prompt, x_sample, state_ret, p_prompt, p_sample, w_in, w_out, norm_pre, norm_post,
          sgu_w, sgu_b, sgu_ln, ret_gn, w_ple_proj, w_ple_gate):
    f = lambda a: np.ascontiguousarray(np.asarray(a, dtype=np.float32))
    x_prompt, x_sample, state_ret = f(x_prompt), f(x_sample), f(state_ret)
    p_prompt, p_sample = f(p_prompt), f(p_sample)
    w_in, w_out, w_ple_proj, w_ple_gate = f(w_in)[0], f(w_out)[0], f(w_ple_proj)[0], f(w_ple_gate)[0]
    norm_pre, norm_post, sgu_w, sgu_b = f(norm_pre)[0], f(norm_post)[0], f(sgu_w)[0], f(sgu_b)[0]
    sgu_ln, ret_gn = f(sgu_ln)[0], f(ret_gn)[0]

    cf, cs, rope_all, e40 = _host_consts()
    gT = np.ascontiguousarray(norm_pre.reshape(8, 128).T)
    gvec = np.ascontiguousarray(np.broadcast_to(np.concatenate([norm_post, sgu_ln, ret_gn])[None, :], (128, 2048)))
    wsT = np.ascontiguousarray(sgu_w.transpose(2, 0, 1))
    wsTs = np.zeros((128, 8, 128), np.float32)
    blk = np.ascontiguousarray(sgu_w[:, :8, :8].transpose(2, 0, 1))
    for b in range(16):
        wsTs[8 * b:8 * b + 8, :, 8 * b:8 * b + 8] = blk
    b40 = np.zeros((40, 256), np.float32)
    b40[0:8, 0:128] = sgu_b
    b40[32:40, 0:128] = sgu_b
    bs = np.tile(sgu_b[:, :8], (1, 16))
    b40[0:8, 128:256] = bs
    b40[32:40, 128:256] = bs

    in_maps = []
    for c in range(NCORES):
        in_maps.append({
            "xp": x_prompt[c], "xs": x_sample[16 * c:16 * c + 16].reshape(128, D),
            "pp": p_prompt[0, c], "psm": p_sample[0, 16 * c:16 * c + 16].reshape(128, PLE),
            "st": state_ret[0, 16 * c:16 * c + 16],
            "w_in": w_in, "w_out": w_out, "w_gate": w_ple_gate, "w_ple": w_ple_proj,
            "gT": gT, "gvec": gvec, "wsT": wsT, "wsTs": wsTs, "b40": b40,
            "cf": cf, "cs": cs, "rope": rope_all, "e40": e40,
        })
    return in_maps


def kernel(**inputs):
    in_maps = _prep(**inputs)
    nc = _get_program()
    res = run_bass_kernel_spmd(nc, in_maps, core_ids=list(range(NCORES)))
    outs = res.results
    y_prompt = np.stack([outs[c]["yp"] for c in range(NCORES)], axis=0)
    y_sample = np.concatenate([outs[c]["ys"].reshape(16, 8, D) for c in range(NCORES)], axis=0)
    st_p = np.stack([outs[c]["sp_out"] for c in range(NCORES)], axis=0)[None]
    st_s = np.concatenate([outs[c]["ss_out"] for c in range(NCORES)], axis=0)[None]
    v_s = np.concatenate([outs[c]["v_out"].reshape(16, 8, 512) for c in range(NCORES)], axis=0)[None]
    return (y_prompt.astype(np.float32), y_sample.astype(np.float32), st_p.astype(np.float32),
            st_s.astype(np.float32), v_s.astype(np.float32))
```

```python
import numpy as np
from contextlib import ExitStack
import concourse.bass as bass
import concourse.mybir as mybir
from concourse.bass_utils import run_bass_kernel_spmd

F32 = mybir.dt.float32
BF16 = mybir.dt.bfloat16
ALU = mybir.AluOpType
AF = mybir.ActivationFunctionType

NCORES = 8
D = 1024
SEQ = 2048
NCH = 16
INW = 3584
PLE = 256
PAST = 16384
RMS_EPS = 1e-6
LN_EPS = 1e-5
GAM = [1.0 - 2.0 ** (-5.0 - h) for h in range(4)]

COMPUTE = ("pe", "act", "dve", "pool")
PSUM_BANKS = ("P01", "P2", "P3", "P4", "P5", "P6", "P7")


class Buf:
    __slots__ = ("name", "w", "r")

    def __init__(self, name):
        self.name = name
        self.w = None
        self.r = []


class Ev:
    __slots__ = ("eng", "idx", "sem", "val", "clock", "needed", "is_dma", "t_done")

    def __init__(self):
        self.eng = None
        self.idx = 0
        self.sem = None
        self.val = 0
        self.clock = None
        self.needed = False
        self.is_dma = False
        self.t_done = 0.0

    def key(self):
        return ("d", id(self.sem)) if self.is_dma else ("e", self.eng)


class Op:
    __slots__ = ("eng", "fn", "deps", "ev", "waits", "slot")


class Sched:
    def __init__(self, nc, stack):
        self.nc = nc
        self.stack = stack
        self.ops = []
        self.per_eng = {e: [] for e in COMPUTE + ("sp",)}
        self.slots = {}
        self.eng_sem = {}
        self.tail = None
        for e in COMPUTE:
            self.eng_sem[e] = stack.enter_context(nc.semaphore("c_" + e))

    def _slot(self, name):
        if name not in self.slots:
            sem = self.stack.enter_context(self.nc.semaphore("d_" + name))
            self.slots[name] = [sem, 0, []]
        return self.slots[name]

    def peek(self, eng, reads=(), writes=(), slot=None):
        return self.add(eng, None, reads, writes, slot, _peek=True)

    def add(self, eng, fn, reads=(), writes=(), slot=None, _peek=False):
        is_dma = slot is not None

        def _flat(xs):
            out = []
            for x in xs:
                if isinstance(x, (list, tuple)):
                    out.extend(_flat(x))
                elif x not in out:
                    out.append(x)
            return out

        reads = _flat(reads)
        writes = _flat(writes)
        deps = []
        for b in reads:
            if b.w is not None:
                deps.append((b.w, "raw"))
            if b.name in PSUM_BANKS:
                for e in b.r:
                    if e.eng != eng:
                        deps.append((e, "bank"))
        for b in writes:
            if b.w is not None:
                deps.append((b.w, "waw"))
            for e in b.r:
                deps.append((e, "war"))
        op = Op()
        op.eng = eng
        op.fn = fn
        op.slot = slot
        fd = []
        seen = set()
        for (e, kind) in deps:
            if id(e) in seen:
                continue
            if (not is_dma) and (not e.is_dma) and e.eng == eng and eng == "pe":
                continue
            seen.add(id(e))
            fd.append(e)
        if _peek:
            return fd
        op.deps = fd
        ev = Ev()
        ev.is_dma = is_dma
        ev.eng = eng
        self.per_eng[eng].append(op)
        if is_dma:
            s = self._slot(slot)
            s[1] += 1
            ev.sem = s[0]
            ev.val = 16 * s[1]
            s[2].append(ev)
        else:
            ev.idx = len(self.per_eng[eng])
            ev.sem = self.eng_sem[eng]
        op.ev = ev
        self.ops.append(op)
        for b in writes:
            b.w = ev
            b.r = []
        for b in reads:
            if b not in writes:
                b.r.append(ev)
        return ev

    def seal_group(self, slot):
        s = self.slots[slot]
        for ev in s[2]:
            ev.val = 16 * s[1]

    def finalize(self):
        known = {e: {} for e in self.per_eng}
        for op in self.ops:
            k = known[op.eng]
            waits = []
            for ev in op.deps:
                key = ev.key()
                val = ev.val if ev.is_dma else ev.idx
                if k.get(key, 0) >= val:
                    continue
                waits.append(ev)
                ev.needed = True
                for kk, vv in ev.clock.items():
                    if k.get(kk, 0) < vv:
                        k[kk] = vv
                if k.get(key, 0) < val:
                    k[key] = val
            op.waits = waits
            op.ev.clock = dict(k)
        for e in COMPUTE:
            c = 0
            for op in self.per_eng[e]:
                if op.ev.is_dma:
                    continue
                if op.ev.needed:
                    c += 1
                    op.ev.val = c

    def emit(self, block):
        sched = self

        def run(engname, engobj):
            for op in sched.per_eng[engname]:
                for ev in op.waits:
                    engobj.wait_ge(ev.sem, ev.val)
                inst = op.fn(engobj)
                if op.ev.is_dma:
                    inst.then_inc(op.ev.sem, 16)
                elif op.ev.needed:
                    inst.then_inc(op.ev.sem, 1)
            if sched.tail and sched.tail[0] == engname:
                for ev in sched.tail[1]:
                    engobj.wait_ge(ev.sem, ev.val)

        @block.sync
        def _(e):
            run("sp", e)

        @block.tensor
        def _(e):
            run("pe", e)

        @block.scalar
        def _(e):
            run("act", e)

        @block.vector
        def _(e):
            run("dve", e)

        @block.gpsimd
        def _(e):
            run("pool", e)


CF = {}
_off = 0
for _n, _w in [("ident", 128), ("wqt", 512), ("dd", 512),
               ("wkv", 4), ("wkvs", 4), ("bm", 16), ("tm", 128), ("tms", 128), ("g8", 512)]:
    CF[_n] = (_off, _w)
    _off += _w
CF_W = _off


def _host_consts():
    cf = np.zeros((128, CF_W), np.float32)
    cs = np.zeros((128, 1024), np.float32)

    def put(name, arr):
        a = np.asarray(arr, np.float64)
        if name == "wqts":
            cs[:, 0:512] = a.reshape(128, 512).astype(np.float32)
            return
        if name == "dds":
            cs[:, 512:1024] = a.reshape(128, 512).astype(np.float32)
            return
        o, w = CF[name]
        cf[:, o:o + w] = a.reshape(128, w).astype(np.float32)

    i = np.arange(128)
    put("ident", np.eye(128))
    g = np.array(GAM, np.float64)
    wqt = g[:, None] ** (i[None, :] + 1.0)
    put("wqt", np.broadcast_to(wqt[None], (128, 4, 128)))
    dd = (128.0 ** -0.5) * g[None, :, None] ** (-(i[:, None, None]) - 1.0) * (i[None, None, :] >= i[:, None, None])
    put("dd", dd)
    put("wkv", (128.0 ** -0.5) * g[None, :] ** (127.0 - i[:, None]))
    t8 = i % 8
    b8 = i // 8
    wqts = g[:, None] ** (t8[None, :] + 1.0)
    put("wqts", np.broadcast_to(wqts[None], (128, 4, 128)))
    same = (b8[:, None] == b8[None, :]) & (t8[None, :] >= t8[:, None])
    dds = (128.0 ** -0.5) * g[None, :, None] ** (-(t8[:, None, None]) - 1.0) * same[:, None, :]
    put("dds", dds)
    put("wkvs", (128.0 ** -0.5) * g[None, :] ** (7.0 - t8[:, None]))
    put("bm", (b8[:, None] == np.arange(16)[None, :]).astype(np.float64))
    put("tm", (i[None, :] >= i[:, None]).astype(np.float64))
    put("tms", same.astype(np.float64))
    put("g8", np.broadcast_to((g ** 8.0)[None, :, None], (128, 4, 128)))
    inv = 10000.0 ** (-(np.arange(64, dtype=np.float64)) / 64.0)

    def rope(pos):
        ang = pos.astype(np.float64)[:, None] * inv[None, :]
        c, s = np.cos(ang), np.sin(ang)
        return np.concatenate([c, c, -s, s], axis=1).astype(np.float32)

    rope_p = rope(np.arange(SEQ)).reshape(NCH, 128, 256)
    rope_s = rope(PAST + t8)
    rope_all = np.ascontiguousarray(np.concatenate([rope_p, rope_s[None]], axis=0))
    e40 = np.zeros((40, 512), np.float32)
    for gi in range(8):
        e40[gi, gi * 64:(gi + 1) * 64] = 1.0
        e40[32 + gi, gi * 64:(gi + 1) * 64] = 1.0
    return cf, cs, rope_all, e40


PSUM_BANKS = ("PA0", "PA1", "PZ0", "PZ1", "PS", "PT", "P6", "P7")
ZORDER = [3, 4, 5, 1, 0, 2, 6]
TRACE_SCHED = None
AGE_BIAS = 0.0
PE_STALL_PEN = 0.0
TIE = 120.0
SLACK = 1.0
SYNC_LAT = 100.0
SAMPLE_POS = 4
WINDOW = 3
DMA_BW = 330.0
DMA_LAT = 2000.0


def build_program(nch=NCH, with_sample=True, window=WINDOW):
    nc = bass.Bass("TRN2", target_bir_lowering=False)

    def din(name, shape):
        return nc.dram_tensor(name, list(shape), F32, kind="ExternalInput").ap()

    def dout(name, shape):
        return nc.dram_tensor(name, list(shape), F32, kind="ExternalOutput").ap()

    xp = din("xp", [SEQ, D])
    xs = din("xs", [128, D])
    pp = din("pp", [SEQ, PLE])
    psm = din("psm", [128, PLE])
    st_in = din("st", [16, 4, 128, 128])
    w_in = din("w_in", [D, INW])
    w_out = din("w_out", [D, D])
    w_gate = din("w_gate", [D, D])
    w_ple = din("w_ple", [PLE, D])
    gT_d = din("gT", [128, 8])
    gvec_d = din("gvec", [128, 2048])
    wsT_d = din("wsT", [128, 8, 128])
    wsTs_d = din("wsTs", [128, 8, 128])
    b40_d = din("b40", [40, 256])
    cf_d = din("cf", [128, CF_W])
    cs_d = din("cs", [128, 1024])
    rope_d = din("rope", [NCH + 1, 128, 256])
    e40_d = din("e40", [40, 512])

    yp = dout("yp", [SEQ, D])
    ys = dout("ys", [128, D])
    sp_out = dout("sp_out", [4, 128, 128])
    ss_out = dout("ss_out", [16, 4, 128, 128])
    v_out = dout("v_out", [128, 512])

    st = ExitStack()
    with st:
        def sb(name, shape, dt=F32):
            return st.enter_context(nc.sbuf_tensor("s_" + name, list(shape), dt))

        def psum(name, shape, dt=F32):
            return st.enter_context(nc.psum_tensor("ps_" + name, list(shape), dt))

        S = Sched(nc, st)
        bufs = {}
        ALIAS = {"vf": ["bufA"], "b40": ["bufA"], "b40f": ["bufA"], "b40h": ["bufA"], "PA": ["PA0", "PA1"], "P67": ["P6", "P7"],
                 "rtA": ["rtA0", "rtA1", "rtA2", "rtA3"], "rtB": ["rtB0", "rtB1"],
                 "ot0": ["rtA0"], "ot1": ["rtA1"], "ot2": ["rtA2"], "ot3": ["rtA3"],
                 "ot": ["rtA0", "rtA1", "rtA2", "rtA3"],
                 "gst": ["gst0", "gst1", "gst2", "gst3"], "gmv": ["gmv0", "gmv1", "gmv2", "gmv3"],
                 "Sst": ["Sst0", "Sst1", "Sst2", "Sst3"], "VM": ["VM0", "VM1", "VM2", "VM3"]}

        def B(name):
            if name in ALIAS:
                return [B(n) for n in ALIAS[name]]
            if name not in bufs:
                bufs[name] = Buf(name)
            return bufs[name]

        win = sb("win", [128, 8, INW], BF16)
        wout = sb("wout", [128, 8, D], BF16)
        wgate = sb("wgate", [128, 8, D], BF16)
        wple = sb("wple", [128, 2, D], BF16)
        shared = sb("shared", [128, 8192], F32)
        cf = sb("cf", [128, CF_W])
        gT = sb("gT", [128, 8])
        gvec = sb("gvec", [128, 2048])
        identb = sb("identb", [128, 128], BF16)
        wsT = sb("wsTb", [128, 8, 128], BF16)
        wsTs = sb("wsTsb", [128, 8, 128], BF16)
        e40 = sb("e40b", [40, 512], BF16)
        b40l = sb("b40l", [40, 256], BF16)
        mhalf = sb("mhalf", [128, 4])

        def cfs(name):
            o, w = CF[name]
            return cf[:, o:o + w]

        xb = [sb("x%d" % i, [128, D]) for i in range(3)]
        pb = [sb("p%d" % i, [128, PLE]) for i in range(2)]
        rb = [sb("rope%d" % i, [128, 256]) for i in range(2)]
        xTb = [sb("xT%d" % i, [128, 8, 128], BF16) for i in range(2)]
        su = sb("su", [128, 512])
        sv = sb("sv", [128, 512])
        sgs = sb("sgs", [128, 512])
        rgs = sb("rgs", [128, 512])
        rtA = sb("rtA", [128, 4, 128])
        ot = rtA
        rtB = sb("rtB", [128, 4, 128])
        Qrb = [sb("Qr%d" % i, [128, 4, 128], BF16) for i in range(2)]
        Krb = [sb("Kr%d" % i, [128, 4, 128], BF16) for i in range(2)]
        Vbb = [sb("Vb%d" % i, [128, 4, 128], BF16) for i in range(2)]
        Vh = sb("Vh", [128, 4, 128], BF16)
        QKT = sb("QKT", [128, 8, 128], BF16)
        scT = sb("scT", [128, 4, 128], BF16)
        vbf = sb("vbf", [128, 512], BF16)
        tsg = sb("tsg", [128, 512])
        mix = sb("mix", [128, D], BF16)
        mixT = sb("mixT", [128, 8, 128], BF16)
        G2 = sb("G2", [128, 512])
        bufA = sb("bufA", [128, D])
        b40 = bufA[0:40, 0:256]
        b40f = bufA[0:40, 256:512]
        b40h = bufA[:].bitcast(BF16)[0:40, 1024:1280]
        vf = bufA[:, 512:1024]
        x1T = sb("x1T", [128, 8, 128], BF16)
        pTb = [sb("pT%d" % i, [128, 2, 128], BF16) for i in range(3)]
        Sst = sb("Sst", [128, 4, 128])
        Sbf = sb("Sbf", [128, 4, 128], BF16)
        ssqb = [sb("ssq_%d" % i, [128, 1]) for i in range(2)]
        msb = [sb("ms_%d" % i, [128, 1]) for i in range(2)]
        rstdb = [sb("rstd_%d" % i, [128, 1]) for i in range(2)]
        ssq2 = sb("ssq2", [128, 1]); ms2 = sb("ms2", [128, 1]); rstd2 = sb("rstd2", [128, 1])
        st6 = sb("st6", [128, 6]); mv = sb("mv", [128, 2]); ve = sb("ve", [128, 1]); rs = sb("rs", [128, 1])
        gst = sb("gst", [128, 4, 6]); gmv = sb("gmv", [128, 4, 2]); gve = sb("gve", [128, 4]); grs = sb("grs", [128, 4])

        PA = psum("PA", [128, 1024])
        PZ = [psum("PZ0", [128, 512]), psum("PZ1", [128, 512])]
        PS = psum("PS", [128, 512])
        PT = psum("PT", [128, 512])
        P67 = psum("P67", [128, 1024])
        PTb = PT[:].bitcast(BF16).rearrange("p (k n) -> p k n", k=8)
        P6 = P67[:, 0:512]
        P7 = P67[:, 512:1024]

        def OP(eng, fn, reads=(), writes=(), slot=None, cost=300.0, nbytes=0, store=False, aset=None):
            return dict(eng=eng, fn=fn, reads=list(reads), writes=list(writes), slot=slot, cost=cost,
                        nbytes=nbytes, store=store, aset=aset)

        c_mm = lambda n: 8.0 + 0.405 * n
        C_TR = 64.0
        c_act = lambda n: 200.0 + 0.87 * n
        c_dve = lambda n: 90.0 + 1.05 * n
        c_pool = lambda n: 150.0 + 2.15 * n
        C_POW = 1000.0

        setup = []
        setup.append(OP("sp", lambda e: e.dma_start(out=cf[:], in_=cf_d), writes=[B("cf")], slot="c_cf", nbytes=128 * CF_W * 4))
        setup.append(OP("sp", lambda e: e.dma_start(out=gT[:], in_=gT_d), writes=[B("gT")], slot="c_gT", nbytes=4096))
        setup.append(OP("sp", lambda e: e.dma_start(out=gvec[:], in_=gvec_d), writes=[B("gvec")], slot="c_gvec", nbytes=1 << 20))
        setup.append(OP("sp", lambda e: e.dma_start(out=b40[:], in_=b40_d), writes=[B("b40")], slot="c_b40", nbytes=40960))
        setup.append(OP("pool", lambda e: e.memset(mhalf[:], -0.5), writes=[B("mhalf")], cost=200))

        stg = [shared[:, i * 2048:(i + 1) * 2048] for i in range(4)]
        stg_cnt = [0]

        def stage_cast(src_ap, dst_ap, shape3, dst_buf):
            i = stg_cnt[0] % 4
            stg_cnt[0] += 1
            k, n = shape3
            sview = stg[i][:, 0:k * n].rearrange("p (k n) -> p k n", k=k)
            setup.append(OP("sp", lambda e: e.dma_start(out=sview, in_=src_ap), writes=[B("stg%d" % i)], slot="stg%d" % i,
                            nbytes=128 * k * n * 4))
            if stg_cnt[0] % 2 == 0:
                setup.append(OP("dve", lambda e: e.tensor_copy(out=dst_ap, in_=sview), reads=[B("stg%d" % i)], writes=[dst_buf],
                                cost=90 + 0.53 * k * n))
            else:
                setup.append(OP("act", lambda e: e.activation(out=dst_ap, in_=sview, func=AF.Copy), reads=[B("stg%d" % i)],
                                writes=[dst_buf], cost=c_act(k * n)))

        setup.append(OP("dve", lambda e: e.tensor_copy(out=identb[:], in_=cfs("ident")), reads=[B("cf")], writes=[B("identb")], cost=200))
        setup.append(OP("dve", lambda e: e.tensor_copy(out=b40h[:], in_=b40[:]), reads=[B("b40")], writes=[B("b40h")], cost=200))
        setup.append(OP("dve", lambda e: e.tensor_copy(out=b40f[:], in_=b40h[:]), reads=[B("b40h")], writes=[B("b40f")], cost=200))
        setup.append(OP("dve", lambda e: e.tensor_copy(out=b40l[0:32, :], in_=b40h[0:32, :]), reads=[B("b40h")], writes=[B("b40l")], cost=200))
        setup.append(OP("dve", lambda e: e.tensor_tensor(out=b40l[32:40, :], in0=b40[32:40, :], in1=b40f[32:40, :], op=ALU.subtract),
                        reads=[B("b40"), B("b40f")], writes=[B("b40l")], cost=200))

        w_in_v = w_in.rearrange("(k p) n -> p k n", p=128)

        def load_ws(src, mask_name, dst, dname):
            i = stg_cnt[0] % 4
            stg_cnt[0] += 1
            sview = stg[i][:, 0:1024].rearrange("p (g n) -> p g n", g=8)
            setup.append(OP("sp", lambda e: e.dma_start(out=sview, in_=src), writes=[B("stg%d" % i)], slot="stg%d" % i, nbytes=1 << 19))
            setup.append(OP("dve", lambda e: e.tensor_tensor(out=dst[:], in0=sview,
                                                             in1=cfs(mask_name).unsqueeze(1).to_broadcast([128, 8, 128]), op=ALU.mult),
                            reads=[B("stg%d" % i), B("cf")], writes=[B(dname)], cost=c_dve(1024)))

        first = True
        for nb in ZORDER:
            for kh in range(2):
                stage_cast(w_in_v[:, 4 * kh:4 * kh + 4, nb * 512:(nb + 1) * 512],
                           win[:, 4 * kh:4 * kh + 4, nb * 512:(nb + 1) * 512], (4, 512), B("win%d" % nb))
            if first:
                first = False
                load_ws(wsT_d, "tm", wsT, "wsT")
                i0 = stg_cnt[0] % 4
                stg_cnt[0] += 1
                e40s = stg[i0][0:40, 0:512]
                setup.append(OP("sp", lambda e: e.dma_start(out=e40s, in_=e40_d), writes=[B("stg%d" % i0)], slot="stg%d" % i0, nbytes=81920))
                setup.append(OP("dve", lambda e: e.tensor_copy(out=e40[:], in_=e40s), reads=[B("stg%d" % i0)], writes=[B("e40")], cost=300))
        w_out_v = w_out.rearrange("(k p) n -> p k n", p=128)
        for c in range(4):
            stage_cast(w_out_v[:, 2 * c:2 * c + 2, :], wout[:, 2 * c:2 * c + 2, :], (2, 1024), B("wout"))
        w_gate_v = w_gate.rearrange("(k p) n -> p k n", p=128)
        for c in range(4):
            stage_cast(w_gate_v[:, 2 * c:2 * c + 2, :], wgate[:, 2 * c:2 * c + 2, :], (2, 1024), B("wgate"))
        w_ple_v = w_ple.rearrange("(k p) n -> p k n", p=128)
        stage_cast(w_ple_v, wple[:], (2, 1024), B("wple"))
        load_ws(wsTs_d, "tms", wsTs, "wsTs")

        shb = shared[:].bitcast(BF16)
        S0f = [shared[:, i * 512:(i + 1) * 512] for i in range(8)]
        S0bf = [shb[:, 8192 + j * 512:8192 + (j + 1) * 512] for j in range(3)]
        oTs = shared[:, 4864:5376]
        VM = shb[:, 10752:12800].rearrange("p (b n) -> p b n", b=4)
        CSB = shared[:, 7168:8192]
        SHARED_ALL = [B("stg%d" % i) for i in range(4)]

        def tile_prog(pos, t):
            ops = []
            A = lambda *a, **k: ops.append(OP(*a, **k))
            sample = (t == NCH)
            xi = pos % 3
            r2 = pos % 2
            X = xb[xi]; BX = B("x%d" % xi)
            Pt = pb[r2]; BP = B("p%d" % r2)
            RP = rb[r2]; BR = B("rope%d" % r2)
            pT = pTb[xi]; BPT = B("pT%d" % xi)
            xT = xTb[r2]; BXT = B("xT%d" % r2)
            Qr = Qrb[r2]; Kr = Krb[r2]; Vb = Vbb[r2]
            ssq = ssqb[r2]; ms = msb[r2]; rstd = rstdb[r2]
            NSSQ, NMS, NRSTD = "ssq_%d" % r2, "ms_%d" % r2, "rstd_%d" % r2
            NQ, NK, NV = "Qr%d" % r2, "Kr%d" % r2, "Vb%d" % r2
            cos2 = RP[:, 0:128]
            sin2 = RP[:, 128:256]
            if not sample:
                A("sp", lambda e: e.dma_start(out=X[:], in_=xp[t * 128:(t + 1) * 128, :]), writes=[BX], slot="x%d" % xi, nbytes=1 << 19, cost=100)
                A("sp", lambda e: e.dma_start(out=Pt[:], in_=pp[t * 128:(t + 1) * 128, :]), writes=[BP], slot="p%d" % r2, nbytes=1 << 17, cost=100)
            else:
                A("sp", lambda e: e.dma_start(out=X[:], in_=xs), writes=[BX], slot="x%d" % xi, nbytes=1 << 19, cost=100)
                A("sp", lambda e: e.dma_start(out=Pt[:], in_=psm), writes=[BP], slot="p%d" % r2, nbytes=1 << 17, cost=100)
            A("sp", lambda e: e.dma_start(out=RP[:], in_=rope_d[t]), writes=[BR], slot="rope%d" % r2, nbytes=1 << 17, cost=100)
            if sample:
                CS = CSB; BCS = B("csb")
                A("sp", lambda e: e.dma_start(out=CS, in_=cs_d), writes=[BCS] + SHARED_ALL, slot="c_cs", nbytes=1 << 19, cost=100)
                WQT = CS[:, 0:512].rearrange("p (h n) -> p h n", h=4)
                DD = CS[:, 512:1024].rearrange("p (h n) -> p h n", h=4)
                BWQ = BCS
                WKV = cfs("wkvs")
                WS = wsTs; BWS = B("wsTs"); bcol = 128
            else:
                WQT = cfs("wqt").rearrange("p (h n) -> p h n", h=4)
                DD = cfs("dd").rearrange("p (h n) -> p h n", h=4)
                BWQ = B("cf")
                WKV = cfs("wkv")
                WS = wsT; BWS = B("wsT"); bcol = 0

            A("act", lambda e: e.activation(out=xT[:].rearrange("p k n -> p (k n)"), in_=X[:], func=AF.Square, accum_out=ssq[:]),
              reads=[BX], writes=[BXT, B(NSSQ)], cost=c_act(1024) + 100)
            A("pool", lambda e: e.tensor_scalar(out=ms[:], in0=ssq[:], scalar1=1.0 / D, scalar2=RMS_EPS, op0=ALU.mult, op1=ALU.add),
              reads=[B(NSSQ)], writes=[B(NMS)], cost=200)
            A("pool", lambda e: e.tensor_tensor(out=rstd[:], in0=ms[:], in1=mhalf[:, 0:1], op=ALU.pow),
              reads=[B(NMS), B("mhalf")], writes=[B(NRSTD)], cost=550)
            PAv = PA[:].rearrange("p (k n) -> p k n", k=8)
            for k in range(8):
                A("pe", lambda e, k=k: e.transpose(out=PAv[:, k, :], in_=X[:, k * 128:(k + 1) * 128], identity=cfs("ident")),
                  reads=[BX, B("cf")], writes=[B("PA")], cost=C_TR)
            A("dve", lambda e: e.tensor_tensor(out=xT[:], in0=PAv, in1=gT[:].unsqueeze(2).to_broadcast([128, 8, 128]), op=ALU.mult),
              reads=[B("PA"), B("gT")], writes=[BXT], cost=c_dve(1024))
            PTf = PT[:, 0:256].rearrange("p (k n) -> p k n", k=2)
            for k in range(2):
                A("pe", lambda e, k=k: e.transpose(out=PTf[:, k, :], in_=Pt[:, k * 128:(k + 1) * 128], identity=cfs("ident")),
                  reads=[BP, B("cf")], writes=[B("PT")], cost=C_TR)
            A("dve", lambda e: e.tensor_copy(out=pT[:], in_=PTf), reads=[B("PT")], writes=[BPT], cost=c_dve(256))

            def zblock(zi, nb):
                zb = PZ[zi % 2][:]
                BZ = B("PZ%d" % (zi % 2))
                for k in range(8):
                    A("pe", lambda e, k=k: e.matmul(out=zb, lhsT=xT[:, k, :], rhs=win[:, k, nb * 512:(nb + 1) * 512],
                                                    start=(k == 0), stop=(k == 7)),
                      reads=[BXT, B("win%d" % nb)], writes=[BZ], cost=c_mm(512))
                return zb, BZ

            for zi, nb in enumerate(ZORDER):
                zb, BZ = zblock(zi, nb)
                zb3 = zb.rearrange("p (h n) -> p h n", h=4)
                if nb == 1:
                    A("act", lambda e, zb=zb: e.activation(out=sv[:], in_=zb, func=AF.Gelu_apprx_tanh, scale=rstd[:]),
                      reads=[BZ, B(NRSTD)], writes=[B("sv")], cost=c_act(512) + 100, aset="gelu")
                    A("dve", lambda e: e.bn_stats(out=st6[:], in_=sv[:]), reads=[B("sv")], writes=[B("st6")], cost=c_dve(512))
                    A("dve", lambda e: e.bn_aggr(out=mv[:], in_=st6[:]), reads=[B("st6")], writes=[B("mv")], cost=200)
                    A("pool", lambda e: e.tensor_scalar(out=ve[:], in0=mv[:, 1:2], scalar1=LN_EPS, scalar2=None, op0=ALU.add),
                      reads=[B("mv")], writes=[B("ve")], cost=200)
                    A("pool", lambda e: e.tensor_tensor(out=rs[:], in0=ve[:], in1=mhalf[:, 0:1], op=ALU.pow),
                      reads=[B("ve"), B("mhalf")], writes=[B("rs")], cost=550)
                    A("dve", lambda e: e.tensor_scalar(out=sv[:], in0=sv[:], scalar1=mv[:, 0:1], scalar2=rs[:], op0=ALU.subtract, op1=ALU.mult),
                      reads=[B("sv"), B("mv"), B("rs")], writes=[B("sv")], cost=c_dve(512))
                    if not sample:
                        A("dve", lambda e: e.tensor_tensor(out=vbf[:], in0=sv[:], in1=gvec[:, 1024:1536], op=ALU.mult),
                          reads=[B("sv"), B("gvec")], writes=[B("vbf")], cost=c_dve(512))
                    else:
                        A("pool", lambda e: e.tensor_tensor(out=vf, in0=sv[:], in1=gvec[:, 1024:1536], op=ALU.mult),
                          reads=[B("sv"), B("gvec")], writes=[B("vf")], cost=c_pool(512))
                        A("sp", lambda e: e.dma_start(out=v_out, in_=vf), reads=[B("vf")], writes=[B("v_out")], slot="o_v",
                          nbytes=1 << 18, cost=100, store=True)
                        A("pool", lambda e: e.tensor_copy(out=vbf[:], in_=vf), reads=[B("vf")], writes=[B("vbf")], cost=1900)
                elif nb == 0:
                    A("act", lambda e, zb=zb: e.activation(out=su[:], in_=zb, func=AF.Gelu_apprx_tanh, scale=rstd[:]),
                      reads=[BZ, B(NRSTD)], writes=[B("su")], cost=c_act(512) + 100, aset="gelu")
                elif nb == 2:
                    A("act", lambda e, zb=zb: e.activation(out=sgs[:], in_=zb, func=AF.Silu, scale=rstd[:]),
                      reads=[BZ, B(NRSTD)], writes=[B("sgs")], cost=c_act(512) + 100, aset="silu")
                    A("pool", lambda e: e.tensor_tensor(out=tsg[:], in0=su[:], in1=sgs[:], op=ALU.mult),
                      reads=[B("su"), B("sgs")], writes=[B("tsg")], cost=c_pool(512))
                elif nb in (3, 4):
                    dst = Qr if nb == 3 else Kr
                    BD = B(NQ) if nb == 3 else B(NK)
                    A("dve", lambda e, zb3=zb3: e.scalar_tensor_tensor(out=rtA[:], in0=zb3, scalar=rstd[:],
                                                                       in1=cos2.unsqueeze(1).to_broadcast([128, 4, 128]),
                                                                       op0=ALU.mult, op1=ALU.mult),
                      reads=[BZ, B(NRSTD), BR], writes=[B("rtA")], cost=c_dve(512))
                    A("dve", lambda e, zb3=zb3: e.scalar_tensor_tensor(out=rtB[:, :, 0:64], in0=zb3[:, :, 64:128], scalar=rstd[:],
                                                                       in1=sin2[:, 0:64].unsqueeze(1).to_broadcast([128, 4, 64]),
                                                                       op0=ALU.mult, op1=ALU.mult),
                      reads=[BZ, B(NRSTD), BR], writes=[B("rtB0")], cost=c_dve(256))
                    A("dve", lambda e, zb3=zb3: e.scalar_tensor_tensor(out=rtB[:, :, 64:128], in0=zb3[:, :, 0:64], scalar=rstd[:],
                                                                       in1=sin2[:, 64:128].unsqueeze(1).to_broadcast([128, 4, 64]),
                                                                       op0=ALU.mult, op1=ALU.mult),
                      reads=[BZ, B(NRSTD), BR], writes=[B("rtB1")], cost=c_dve(256))
                    A("pool", lambda e, dst=dst: e.tensor_tensor(out=dst[:], in0=rtA[:], in1=rtB[:], op=ALU.add),
                      reads=[B("rtA"), B("rtB")], writes=[BD], cost=c_pool(512))
                elif nb == 5:
                    A("act", lambda e, zb3=zb3: e.activation(out=Vb[:], in_=zb3, func=AF.Identity, scale=rstd[:]),
                      reads=[BZ, B(NRSTD)], writes=[B(NV)], cost=c_act(512) + 100)
                    A("pool", lambda e: e.tensor_tensor(out=Vh[:], in0=Vb[:], in1=WKV.unsqueeze(2).to_broadcast([128, 4, 128]), op=ALU.mult),
                      reads=[B(NV), B("cf")], writes=[B("Vh")], cost=1050)
                elif nb == 6:
                    A("act", lambda e, zb=zb: e.activation(out=rgs[:], in_=zb, func=AF.Silu, scale=rstd[:]),
                      reads=[BZ, B(NRSTD)], writes=[B("rgs")], cost=c_act(512) + 100, aset="silu")
                    A("pool", lambda e: e.tensor_tensor(out=G2[:], in0=rgs[:], in1=gvec[:, 1536:2048], op=ALU.mult),
                      reads=[B("rgs"), B("gvec")], writes=[B("G2")], cost=c_pool(512))
                if nb == 2:
                    A("pe", lambda e: e.matmul(out=PS[:], lhsT=b40l[:, bcol:bcol + 128], rhs=e40[:], start=True, stop=False),
                      reads=[B("b40l"), B("e40")], writes=[B("PS")], cost=c_mm(512))
                    for g in range(8):
                        A("pe", lambda e, g=g: e.matmul(out=PS[:, g * 64:(g + 1) * 64], lhsT=WS[:, g, :], rhs=vbf[:, g * 64:(g + 1) * 64],
                                                        start=False, stop=(g == 7)),
                          reads=[BWS, B("vbf")], writes=[B("PS")], cost=c_mm(64))
                    A("dve", lambda e: e.tensor_tensor(out=mix[:, 0:512], in0=tsg[:], in1=PS[:], op=ALU.mult),
                      reads=[B("tsg"), B("PS")], writes=[B("mixA")], cost=c_dve(512))

            for h in range(4):
                A("pe", lambda e, h=h: e.transpose(out=PTb[:, h, :], in_=Qr[:, h, :], identity=identb[:]),
                  reads=[B(NQ), B("identb")], writes=[B("PT")], cost=C_TR)
            for h in range(4):
                A("pe", lambda e, h=h: e.transpose(out=PTb[:, 4 + h, :], in_=Kr[:, h, :], identity=identb[:]),
                  reads=[B(NK), B("identb")], writes=[B("PT")], cost=C_TR)
            A("dve", lambda e: e.tensor_tensor(out=QKT[:, 0:4, :], in0=PTb[:, 0:4, :], in1=WQT, op=ALU.mult),
              reads=[B("PT"), BWQ], writes=[B("QT")], cost=c_dve(512))
            A("act", lambda e: e.activation(out=QKT[:, 4:8, :], in_=PTb[:, 4:8, :], func=AF.Copy),
              reads=[B("PT")], writes=[B("KT")], cost=c_act(512))
            PSv = PS[:].rearrange("p (h n) -> p h n", h=4)
            P6v = P6.rearrange("p (h n) -> p h n", h=4)
            P7v = P7.rearrange("p (h n) -> p h n", h=4)
            for h in range(4):
                A("pe", lambda e, h=h: e.matmul(out=PSv[:, h, :], lhsT=QKT[:, 4 + h, :], rhs=QKT[:, h, :], start=True, stop=True),
                  reads=[B("QT"), B("KT")], writes=[B("PS")], cost=c_mm(128))
            A("dve", lambda e: e.tensor_tensor(out=scT[:], in0=PSv, in1=DD, op=ALU.mult),
              reads=[B("PS"), BWQ], writes=[B("scT")], cost=c_dve(512))

            if not sample:
                for h in range(4):
                    A("pe", lambda e, h=h: e.matmul(out=P6v[:, h, :], lhsT=Kr[:, h, :], rhs=Vh[:, h, :], start=True, stop=True),
                      reads=[B(NK), B("Vh")], writes=[B("P6")], cost=c_mm(128))
                for h in range(4):
                    A("pe", lambda e, h=h: e.matmul(out=P7v[:, h, :], lhsT=scT[:, h, :], rhs=Vb[:, h, :], start=True, stop=(t == 0)),
                      reads=[B("scT"), B(NV)], writes=[B("P7")], cost=c_mm(128))
                    if t > 0:
                        A("pe", lambda e, h=h: e.matmul(out=P7v[:, h, :], lhsT=QKT[:, h, :], rhs=Sbf[:, h, :], start=False, stop=True),
                          reads=[B("QT"), B("Sbf")], writes=[B("P7")], cost=c_mm(128))
                if t == 0:
                    A("dve", lambda e: e.tensor_copy(out=Sst[:], in_=P6v), reads=[B("P6")], writes=[B("Sst")], cost=c_dve(512))
                else:
                    for h in range(4):
                        A("dve", lambda e, h=h: e.scalar_tensor_tensor(out=Sst[:, h, :], in0=Sst[:, h, :], scalar=float(GAM[h] ** 128),
                                                                       in1=P6v[:, h, :], op0=ALU.mult, op1=ALU.add),
                          reads=[B("P6"), B("Sst%d" % h)], writes=[B("Sst%d" % h)], cost=300)
                if t < nch - 1:
                    A("act", lambda e: e.activation(out=Sbf[:], in_=Sst[:], func=AF.Copy), reads=[B("Sst")], writes=[B("Sbf")], cost=c_act(512))
                else:
                    A("sp", lambda e: e.dma_start(out=sp_out.rearrange("h d e -> d h e"), in_=Sst[:]),
                      reads=[B("Sst")], writes=[B("sp_out")], slot="o_sp", nbytes=1 << 18, cost=100, store=True)
            else:
                G8 = cfs("g8")
                BMc = cfs("bm")
                for b in range(16):
                    ks = b % 8
                    js = b % 3
                    S0 = S0f[ks]
                    BS0 = B("S0f%d" % ks)
                    S0b = S0bf[js]
                    BS0b = B("S0bf%d" % js)
                    S0v = S0.rearrange("p (h n) -> p h n", h=4)
                    A("sp", lambda e, b=b, S0v=S0v: e.dma_start(out=S0v, in_=st_in[b].rearrange("h d e -> d h e")),
                      writes=[BS0] + (SHARED_ALL if b < 8 else []), slot="s0_%d" % ks, nbytes=1 << 18, cost=100)
                    A("act", lambda e, S0=S0, S0b=S0b: e.activation(out=S0b, in_=S0, func=AF.Copy),
                      reads=[BS0], writes=[BS0b] + (SHARED_ALL if b < 3 else []), cost=c_act(512))
                    if b % 4 == 0:
                        for bb in range(4):
                            A("dve", lambda e, b=b, bb=bb: e.tensor_scalar(out=VM[:, bb, :], in0=Vh[:].rearrange("p h n -> p (h n)"),
                                                                           scalar1=BMc[:, b + bb:b + bb + 1], scalar2=None, op0=ALU.mult),
                              reads=[B("Vh"), B("cf")], writes=[B("VM%d" % bb)] + (SHARED_ALL if b == 0 else []), cost=300)
                    PU = PZ[b % 2][:]
                    BPU = B("PZ%d" % (b % 2))
                    PUv = PU.rearrange("p (h n) -> p h n", h=4)
                    VMv = VM[:, b % 4, :].rearrange("p (h n) -> p h n", h=4)
                    for h in range(4):
                        A("pe", lambda e, h=h, PUv=PUv, VMv=VMv: e.matmul(out=PUv[:, h, :], lhsT=Kr[:, h, :], rhs=VMv[:, h, :], start=True, stop=True),
                          reads=[B(NK), B("VM%d" % (b % 4))], writes=[BPU], cost=c_mm(128))
                    for h in range(4):
                        A("pe", lambda e, h=h, b=b, S0b=S0b: e.matmul(out=P6[:, h * 128 + 8 * b:h * 128 + 8 * b + 8],
                                                                      lhsT=S0b[:, h * 128:(h + 1) * 128], rhs=QKT[:, h, 8 * b:8 * b + 8],
                                                                      start=True, stop=True),
                          reads=[BS0b, B("QT")], writes=[B("P6")], cost=70)
                    A("pool", lambda e, S0=S0: e.tensor_tensor(out=S0, in0=S0, in1=G8, op=ALU.mult),
                      reads=[BS0, B("cf")], writes=[BS0], cost=c_pool(512))
                    A("dve", lambda e, S0=S0, PU=PU: e.tensor_tensor(out=S0, in0=S0, in1=PU, op=ALU.add),
                      reads=[BS0, BPU], writes=[BS0], cost=c_dve(512))
                    A("sp", lambda e, b=b, S0v=S0v: e.dma_start(out=ss_out[b].rearrange("h d e -> d h e"), in_=S0v),
                      reads=[BS0], writes=[B("ss_out")], slot="o_ss%d" % ks, nbytes=1 << 18, cost=100, store=True)
                A("act", lambda e: e.activation(out=oTs, in_=P6, func=AF.Copy), reads=[B("P6")], writes=[B("oTs")] + SHARED_ALL, cost=c_act(512))
                for h in range(4):
                    A("pe", lambda e, h=h: e.matmul(out=P7v[:, h, :], lhsT=scT[:, h, :], rhs=Vb[:, h, :], start=True, stop=False),
                      reads=[B("scT"), B(NV)], writes=[B("P7")], cost=c_mm(128))
                    A("pe", lambda e, h=h: e.matmul(out=P7v[:, h, :], lhsT=oTs[:, h * 128:(h + 1) * 128], rhs=cfs("ident"), start=False, stop=True),
                      reads=[B("oTs"), B("cf")], writes=[B("P7")], cost=4 * c_mm(128))

            for h in range(4):
                A("dve", lambda e, h=h: e.bn_stats(out=gst[:, h, :], in_=P7v[:, h, :]), reads=[B("P7")], writes=[B("gst%d" % h)], cost=210)
            for h in range(4):
                A("dve", lambda e, h=h: e.bn_aggr(out=gmv[:, h, :], in_=gst[:, h, :]), reads=[B("gst%d" % h)], writes=[B("gmv%d" % h)], cost=100)
            A("pool", lambda e: e.tensor_scalar(out=gve[:], in0=gmv[:, :, 1], scalar1=LN_EPS, scalar2=None, op0=ALU.add),
              reads=[B("gmv")], writes=[B("gve")], cost=200)
            A("pool", lambda e: e.tensor_tensor(out=grs[:], in0=gve[:], in1=mhalf[:], op=ALU.pow),
              reads=[B("gve"), B("mhalf")], writes=[B("grs")], cost=C_POW)
            for h in range(4):
                A("dve", lambda e, h=h: e.tensor_scalar(out=ot[:, h, :], in0=P7v[:, h, :], scalar1=gmv[:, h, 0:1], scalar2=grs[:, h:h + 1],
                                                        op0=ALU.subtract, op1=ALU.mult),
                  reads=[B("P7"), B("gmv%d" % h), B("grs")], writes=[B("ot%d" % h)], cost=340)
            A("pool", lambda e: e.tensor_tensor(out=mix[:, 512:1024], in0=ot[:].rearrange("p h n -> p (h n)"), in1=G2[:], op=ALU.mult),
              reads=[B("ot"), B("G2")], writes=[B("mixB")], cost=c_pool(512))

            for k in range(8):
                A("pe", lambda e, k=k: e.transpose(out=PTb[:, k, :], in_=mix[:, k * 128:(k + 1) * 128], identity=identb[:]),
                  reads=[B("mixA"), B("mixB"), B("identb")], writes=[B("PT")], cost=C_TR)
            A("act", lambda e: e.activation(out=mixT[:], in_=PTb, func=AF.Copy), reads=[B("PT")], writes=[B("mixT")], cost=c_act(1024))
            for n in range(2):
                for k in range(8):
                    A("pe", lambda e, k=k, n=n: e.matmul(out=P67[:, n * 512:(n + 1) * 512], lhsT=mixT[:, k, :], rhs=wout[:, k, n * 512:(n + 1) * 512],
                                                         start=(k == 0), stop=(k == 7)),
                      reads=[B("mixT"), B("wout")], writes=[B("P6") if n == 0 else B("P7")], cost=c_mm(512))
            A("act", lambda e: e.activation(out=bufA[:], in_=P67[:], func=AF.Square, accum_out=ssq2[:]),
              reads=[B("P67")], writes=[B("bufA"), B("ssq2")], cost=c_act(1024) + 100)
            A("pool", lambda e: e.tensor_scalar(out=ms2[:], in0=ssq2[:], scalar1=1.0 / D, scalar2=RMS_EPS, op0=ALU.mult, op1=ALU.add),
              reads=[B("ssq2")], writes=[B("ms2")], cost=200)
            A("pool", lambda e: e.tensor_tensor(out=rstd2[:], in0=ms2[:], in1=mhalf[:, 0:1], op=ALU.pow),
              reads=[B("ms2"), B("mhalf")], writes=[B("rstd2")], cost=550)
            A("dve", lambda e: e.scalar_tensor_tensor(out=bufA[:], in0=P67[:], scalar=rstd2[:], in1=gvec[:, 0:1024], op0=ALU.mult, op1=ALU.mult),
              reads=[B("P67"), B("rstd2"), B("gvec")], writes=[B("bufA")], cost=c_dve(1024))
            A("dve", lambda e: e.tensor_tensor(out=X[:], in0=X[:], in1=bufA[:], op=ALU.add),
              reads=[BX, B("bufA")], writes=[BX], cost=c_dve(1024))
            for k in range(8):
                A("pe", lambda e, k=k: e.transpose(out=PAv[:, k, :], in_=X[:, k * 128:(k + 1) * 128], identity=cfs("ident")),
                  reads=[BX, B("cf")], writes=[B("PA")], cost=C_TR)
            A("act", lambda e: e.activation(out=x1T[:], in_=PAv, func=AF.Copy), reads=[B("PA")], writes=[B("x1T")], cost=c_act(1024))
            for n in range(2):
                for k in range(8):
                    A("pe", lambda e, k=k, n=n: e.matmul(out=P67[:, n * 512:(n + 1) * 512], lhsT=x1T[:, k, :], rhs=wgate[:, k, n * 512:(n + 1) * 512],
                                                         start=(k == 0), stop=(k == 7)),
                      reads=[B("x1T"), B("wgate")], writes=[B("P6") if n == 0 else B("P7")], cost=c_mm(512))
            for n in range(2):
                for k in range(2):
                    A("pe", lambda e, k=k, n=n: e.matmul(out=PA[:, n * 512:(n + 1) * 512], lhsT=pT[:, k, :], rhs=wple[:, k, n * 512:(n + 1) * 512],
                                                         start=(k == 0), stop=(k == 1)),
                      reads=[BPT, B("wple")], writes=[B("PA")], cost=c_mm(512))
            A("act", lambda e: e.activation(out=bufA[:], in_=P67[:], func=AF.Tanh, scale=0.5), reads=[B("P67")], writes=[B("bufA")], cost=c_act(1024))
            A("dve", lambda e: e.scalar_tensor_tensor(out=bufA[:], in0=bufA[:], scalar=1.0, in1=PA[:], op0=ALU.add, op1=ALU.mult),
              reads=[B("bufA"), B("PA")], writes=[B("bufA")], cost=c_dve(1024))
            A("dve", lambda e: e.scalar_tensor_tensor(out=X[:], in0=bufA[:], scalar=0.5, in1=X[:], op0=ALU.mult, op1=ALU.add),
              reads=[B("bufA"), BX], writes=[BX], cost=c_dve(1024))
            dsty = ys if sample else yp[t * 128:(t + 1) * 128, :]
            A("sp", lambda e: e.dma_start(out=dsty, in_=X[:]), reads=[BX], writes=[B("y_out%d" % xi)], slot="oy%d" % xi,
              nbytes=1 << 19, cost=100, store=True)
            return ops

        tiles = list(range(nch))
        if with_sample:
            tiles.insert(min(SAMPLE_POS, nch), NCH)
        progs = [setup] + [tile_prog(i, t) for i, t in enumerate(tiles)]

        def _flatb(xs):
            out = []
            for x in xs:
                if isinstance(x, (list, tuple)):
                    out.extend(_flatb(x))
                else:
                    out.append(x)
            return out

        for prog in progs:
            for d in prog:
                d["reads"] = _flatb(d["reads"])
                d["writes"] = _flatb(d["writes"])
                d["rset"] = set(id(b) for b in d["reads"])
                d["wset"] = set(id(b) for b in d["writes"])

        npred = []
        succs = []
        for prog in progs:
            lastw = {}
            readers = {}
            np_ = [0] * len(prog)
            sc = [[] for _ in prog]
            for i, d in enumerate(prog):
                ps = set()
                for bid in d["rset"]:
                    if bid in lastw:
                        ps.add(lastw[bid])
                for bid in d["wset"]:
                    if bid in lastw:
                        ps.add(lastw[bid])
                    for j in readers.get(bid, ()):
                        ps.add(j)
                ps.discard(i)
                if prog is setup and i > 0:
                    ps.add(i - 1)
                for j in ps:
                    sc[j].append(i)
                np_[i] = len(ps)
                for bid in d["wset"]:
                    lastw[bid] = i
                    readers[bid] = []
                for bid in d["rset"]:
                    if bid not in d["wset"]:
                        readers.setdefault(bid, []).append(i)
            npred.append(np_)
            succs.append(sc)
        ready = [sorted(i for i in range(len(p)) if npred[k][i] == 0) for k, p in enumerate(progs)]
        left = [len(p) for p in progs]

        pending = {}
        for d in setup:
            for bid in d["wset"]:
                pending[bid] = pending.get(bid, 0) + 1

        FLOW = set(id(b) for b in _flatb([B("Sst"), B("Sbf")]))
        tile_written = set()
        for pi in range(1, len(progs)):
            for d in progs[pi]:
                tile_written |= d["wset"]
        seglen = {}
        segops = {}
        flow_left = {}
        for pi in range(1, len(progs)):
            acc = {}
            for oi, d in enumerate(progs[pi]):
                for bid in d["rset"] | d["wset"]:
                    if bid in tile_written:
                        acc.setdefault(bid, []).append((oi, bid in d["rset"], bid in d["wset"]))
            for bid, lst in acc.items():
                if bid in FLOW:
                    flow_left.setdefault(bid, {})[pi] = len(lst)
                    continue
                st0 = 0
                had_read = False
                for k, (oi, isr, isw) in enumerate(lst):
                    fresh = isw and not isr
                    if k > 0 and fresh and had_read:
                        for kk in range(st0, k):
                            seglen[(pi, lst[kk][0], bid)] = k - st0
                            segops[(pi, lst[kk][0], bid)] = [x[0] for x in lst[st0:k]]
                        st0 = k
                        had_read = False
                    if isr:
                        had_read = True
                for kk in range(st0, len(lst)):
                    seglen[(pi, lst[kk][0], bid)] = len(lst) - st0
                    segops[(pi, lst[kk][0], bid)] = [x[0] for x in lst[st0:]]
        holder = {}

        bank_ids = set(id(bufs[n]) for n in PSUM_BANKS if n in bufs)

        def eligible(pi, oi, d):
            for bid in d["rset"] | d["wset"]:
                if pending.get(bid, 0) > 0:
                    return False
                if bid not in tile_written:
                    continue
                if bid in FLOW:
                    for pj, lf in flow_left[bid].items():
                        if pj < pi and lf > 0:
                            return False
                else:
                    h = holder.get(bid)
                    if h is not None and h[0] != pi:
                        return False
                    if h is None and bid in bank_ids:
                        for oj in segops[(pi, oi, bid)]:
                            dj = progs[pi][oj]
                            for b2 in dj["rset"] | dj["wset"]:
                                h2 = holder.get(b2)
                                if h2 is not None and h2[0] != pi:
                                    return False
            return True

        eng_free = {e: 0.0 for e in COMPUTE + ("sp",)}
        dma_free = [0.0]
        act_set = [None]
        store_events = []
        active = [0] + list(range(1, min(len(progs), 1 + window)))
        nxt = 1 + window
        while active:
            best = None
            best_key = None
            for pi in active:
                for oi in ready[pi]:
                    d = progs[pi][oi]
                    if pi != 0 and not eligible(pi, oi, d):
                        continue
                    deps = S.peek(d["eng"], d["reads"], d["writes"], d["slot"])
                    t_ready = max([ev.t_done for ev in deps], default=0.0)
                    start = max(t_ready, eng_free[d["eng"]])
                    key = (start + AGE_BIAS * (pi - active[1 if (len(active) > 1 and active[0] == 0) else 0]), pi, oi)
                    if best is None or key[0] < best_key[0] - TIE or (abs(key[0] - best_key[0]) <= TIE and key[1:] < best_key[1:]):
                        best, best_key, best_start = (pi, oi), key, start
                    if pi == 0:
                        break
            if best is None:
                names = {id(b): n for n, b in bufs.items()}
                msg = []
                for pi in active:
                    for oi in ready[pi][:6]:
                        d = progs[pi][oi]
                        why = []
                        for bid in d["rset"] | d["wset"]:
                            if pending.get(bid, 0) > 0:
                                why.append("pending:" + names.get(bid, "?"))
                            h = holder.get(bid)
                            if h is not None and h[0] != pi:
                                why.append("held:%s by %d (%d left)" % (names.get(bid, "?"), h[0], h[1]))
                            if bid in FLOW:
                                why.append("flow:" + names.get(bid, "?") + str(flow_left[bid]))
                        msg.append("tile %d op %d %s: %s" % (pi, oi, d["eng"], why))
                raise RuntimeError("list scheduler deadlock\n" + "\n".join(msg))
            pi, oi = best
            d = progs[pi][oi]
            if pi == 0:
                for bid in d["wset"]:
                    pending[bid] -= 1
            else:
                for bid in d["rset"] | d["wset"]:
                    if bid not in tile_written:
                        continue
                    if bid in FLOW:
                        flow_left[bid][pi] -= 1
                    else:
                        h = holder.get(bid)
                        if h is None:
                            h = holder[bid] = [pi, seglen[(pi, oi, bid)]]
                        h[1] -= 1
                        if h[1] == 0:
                            del holder[bid]
            cost = d["cost"]
            if d["eng"] == "pe" and best_start > eng_free["pe"] + 40.0:
                cost = cost + PE_STALL_PEN
            if d.get("aset") is not None and d["eng"] == "act":
                if act_set[0] is not None and d["aset"] != act_set[0]:
                    cost = cost + 1300.0
                act_set[0] = d["aset"]
            if TRACE_SCHED is not None:
                TRACE_SCHED.append((pi, oi, d["eng"], best_start, cost, eng_free[d["eng"]], None, None))
            ev = S.add(d["eng"], d["fn"], d["reads"], d["writes"], d["slot"])
            if d["slot"] is not None:
                eng_free[d["eng"]] = best_start + cost
                xfer = d["nbytes"] / DMA_BW
                t0 = max(best_start, dma_free[0])
                dma_free[0] = t0 + xfer
                ev.t_done = t0 + xfer + DMA_LAT
            else:
                eng_free[d["eng"]] = best_start + cost
                ev.t_done = best_start + cost * SLACK + SYNC_LAT
            if d["store"]:
                store_events.append(ev)
            ready[pi].remove(oi)
            left[pi] -= 1
            for j in succs[pi][oi]:
                npred[pi][j] -= 1
                if npred[pi][j] == 0:
                    ready[pi].append(j)
            ready[pi].sort()
            if left[pi] == 0:
                active.remove(pi)
                if nxt < len(progs):
                    active.append(nxt)
                    nxt += 1
        build_program.est_ns = max(eng_free.values())

        S.finalize()
        S.tail = ("sp", store_events)
        with nc.Block() as block:
            S.emit(block)
    return nc


_PROG = {}


def _get_program():
    if "nc" not in _PROG:
        _PROG["nc"] = build_program()
    return _PROG["nc"]


def _prep(x_prompt, x_sample, state_ret, p_prompt, p_sample, w_in, w_out, norm_pre, norm_post,
          sgu_w, sgu_b, sgu_ln, ret_gn, w_ple_proj, w_ple_gate):
    f = lambda a: np.ascontiguousarray(np.asarray(a, dtype=np.float32))
    x_prompt, x_sample, state_ret = f(x_prompt), f(x_sample), f(state_ret)
    p_prompt, p_sample = f(p_prompt), f(p_sample)
    w_in, w_out, w_ple_proj, w_ple_gate = f(w_in)[0], f(w_out)[0], f(w_ple_proj)[0], f(w_ple_gate)[0]
    norm_pre, norm_post, sgu_w, sgu_b = f(norm_pre)[0], f(norm_post)[0], f(sgu_w)[0], f(sgu_b)[0]
    sgu_ln, ret_gn = f(sgu_ln)[0], f(ret_gn)[0]

    cf, cs, rope_all, e40 = _host_consts()
    gT = np.ascontiguousarray(norm_pre.reshape(8, 128).T)
    gvec = np.ascontiguousarray(np.broadcast_to(np.concatenate([norm_post, sgu_ln, ret_gn])[None, :], (128, 2048)))
    wsT = np.ascontiguousarray(sgu_w.transpose(2, 0, 1))
    wsTs = np.zeros((128, 8, 128), np.float32)
    blk = np.ascontiguousarray(sgu_w[:, :8, :8].transpose(2, 0, 1))
    for b in range(16):
        wsTs[8 * b:8 * b + 8, :, 8 * b:8 * b + 8] = blk
    b40 = np.zeros((40, 256), np.float32)
    b40[0:8, 0:128] = sgu_b
    b40[32:40, 0:128] = sgu_b
    bs = np.tile(sgu_b[:, :8], (1, 16))
    b40[0:8, 128:256] = bs
    b40[32:40, 128:256] = bs

    in_maps = []
    for c in range(NCORES):
        in_maps.append({
            "xp": x_prompt[c], "xs": x_sample[16 * c:16 * c + 16].reshape(128, D),
            "pp": p_prompt[0, c], "psm": p_sample[0, 16 * c:16 * c + 16].reshape(128, PLE),
            "st": state_ret[0, 16 * c:16 * c + 16],
            "w_in": w_in, "w_out": w_out, "w_gate": w_ple_gate, "w_ple": w_ple_proj,
            "gT": gT, "gvec": gvec, "wsT": wsT, "wsTs": wsTs, "b40": b40,
            "cf": cf, "cs": cs, "rope": rope_all, "e40": e40,
        })
    return in_maps


def kernel(**inputs):
    in_maps = _prep(**inputs)
    nc = _get_program()
    res = run_bass_kernel_spmd(nc, in_maps, core_ids=list(range(NCORES)))
    outs = res.results
    y_prompt = np.stack([outs[c]["yp"] for c in range(NCORES)], axis=0)
    y_sample = np.concatenate([outs[c]["ys"].reshape(16, 8, D) for c in range(NCORES)], axis=0)
    st_p = np.stack([outs[c]["sp_out"] for c in range(NCORES)], axis=0)[None]
    st_s = np.concatenate([outs[c]["ss_out"] for c in range(NCORES)], axis=0)[None]
    v_s = np.concatenate([outs[c]["v_out"].reshape(16, 8, 512) for c in range(NCORES)], axis=0)[None]
    return (y_prompt.astype(np.float32), y_sample.astype(np.float32), st_p.astype(np.float32),
            st_s.astype(np.float32), v_s.astype(np.float32))
```
